# Optimizing a Trainium2 kernel written in Bass

```python
import jax, jax.numpy as jnp
from jax import lax
import numpy as np

D_MODEL = 2048
BATCH = 2
SEQ = 4096
DEPTH = 1

HEAD_DIM = 128
ATTN_WIDTH = D_MODEL // 2
CONV_WIDTH = D_MODEL - ATTN_WIDTH
N_Q_HEADS = ATTN_WIDTH // HEAD_DIM
N_KV_HEADS = max(1, N_Q_HEADS // 4)
KV_WIDTH = N_KV_HEADS * HEAD_DIM
CONV_GROUPS = CONV_WIDTH // HEAD_DIM
WINDOW = 128
BLOCK = 128
CONV_K = 3
D_FF = ((8 * D_MODEL // 3 + 255) // 256) * 256
EPS = 1e-6
NEG_INF = -1e30
SPLIT_SIZES = (ATTN_WIDTH, KV_WIDTH, KV_WIDTH, CONV_WIDTH, CONV_WIDTH, CONV_WIDTH)
IN_WIDTH = sum(SPLIT_SIZES)

kernel_name = "hymba_swa_alibi_shortconv_convffn_encoder"


def rms_norm(x, w):
    xf = x.astype(jnp.float32)
    y = xf * lax.rsqrt(jnp.mean(xf * xf, axis=-1, keepdims=True) + EPS)
    return (y * w.astype(jnp.float32)).astype(x.dtype)


def dwconv3(u, w, b):
    up = jnp.pad(u, ((0, 0), (1, 1), (0, 0)))
    return up[:, :-2] * w[0] + up[:, 1:-1] * w[1] + up[:, 2:] * w[2] + b


def alibi_slopes(n_heads):
    return jnp.asarray(2.0 ** (-8.0 * np.arange(1, n_heads + 1) / n_heads), dtype=jnp.float32)


def banded_attention(q, k, v, sinks):
    bsz, s = q.shape[0], q.shape[1]
    nb = s // BLOCK
    g = N_Q_HEADS // N_KV_HEADS
    qb = q.reshape(bsz, nb, BLOCK, N_KV_HEADS, g, HEAD_DIM)

    def band(t):
        tp = jnp.pad(t, ((0, 0), (BLOCK, BLOCK), (0, 0), (0, 0)))
        tp = tp.reshape(bsz, nb + 2, BLOCK, N_KV_HEADS, HEAD_DIM)
        return jnp.concatenate([tp[:, :-2], tp[:, 1:-1], tp[:, 2:]], axis=2)

    kb, vb = band(k), band(v)
    scale = HEAD_DIM ** -0.5
    scores = jnp.einsum('bnqhgd,bnkhd->bnhgqk', qb, kb,
                        preferred_element_type=jnp.float32) * scale

    qi = jnp.arange(BLOCK)[:, None]
    kj = jnp.arange(3 * BLOCK)[None, :]
    rel = kj - BLOCK - qi
    s_pos = jnp.arange(nb)[:, None, None] * BLOCK + kj[None] - BLOCK
    valid = (jnp.abs(rel) <= WINDOW)[None] & (s_pos >= 0) & (s_pos < s)

    slopes = alibi_slopes(N_Q_HEADS).reshape(N_KV_HEADS, g)
    bias = -slopes[:, :, None, None] * jnp.abs(rel).astype(jnp.float32)
    scores = jnp.where(valid[None, :, None, None], scores + bias[None, None], NEG_INF)

    sink = jnp.broadcast_to(sinks.astype(jnp.float32).reshape(1, 1, N_KV_HEADS, g, 1, 1),
                            scores.shape[:-1] + (1,))
    probs = jax.nn.softmax(jnp.concatenate([scores, sink], axis=-1), axis=-1)[..., :-1]
    out = jnp.einsum('bnhgqk,bnkhd->bnqhgd', probs.astype(v.dtype), vb)
    return out.reshape(bsz, s, ATTN_WIDTH)


def setup_inputs(seed: int = 0) -> dict:
    key = jax.random.key(seed)
    ks = jax.random.split(key, 17)
    f32 = jnp.float32
    nrm = lambda k, shape, sc: jax.random.normal(k, shape, f32) * sc
    gain = lambda k, shape: 1.0 + 0.02 * jax.random.normal(k, shape, f32)
    return {
        "x": jax.random.normal(ks[0], (BATCH, SEQ, D_MODEL), f32),
        "attn_norm_w": gain(ks[1], (DEPTH, D_MODEL)),
        "w_in": nrm(ks[2], (DEPTH, D_MODEL, IN_WIDTH), D_MODEL ** -0.5),
        "sink_logits": nrm(ks[3], (DEPTH, N_Q_HEADS), 0.5),
        "mix_conv_w": nrm(ks[4], (DEPTH, CONV_K, CONV_WIDTH), CONV_K ** -0.5),
        "mix_conv_b": nrm(ks[5], (DEPTH, CONV_WIDTH), 0.02),
        "attn_out_norm_w": gain(ks[6], (DEPTH, ATTN_WIDTH)),
        "conv_out_norm_w": gain(ks[7], (DEPTH, CONV_WIDTH)),
        "w_out": nrm(ks[8], (DEPTH, D_MODEL, D_MODEL), D_MODEL ** -0.5),
        "ffn_norm_w": gain(ks[9], (DEPTH, D_MODEL)),
        "w_gate": nrm(ks[10], (DEPTH, D_MODEL, D_FF), D_MODEL ** -0.5),
        "w_up": nrm(ks[11], (DEPTH, D_MODEL, D_FF), D_MODEL ** -0.5),
        "ffn_conv_w": nrm(ks[12], (DEPTH, CONV_K, D_FF), CONV_K ** -0.5),
        "ffn_conv_b": nrm(ks[13], (DEPTH, D_FF), 0.02),
        "w_down": nrm(ks[14], (DEPTH, D_FF, D_MODEL), D_FF ** -0.5),
        "final_norm_w": gain(ks[15], (D_MODEL,)),
    }


def reference(x, attn_norm_w, w_in, sink_logits, mix_conv_w, mix_conv_b,
              attn_out_norm_w, conv_out_norm_w, w_out, ffn_norm_w, w_gate, w_up,
              ffn_conv_w, ffn_conv_b, w_down, final_norm_w):
    bsz, s = x.shape[0], x.shape[1]
    split_idx = [int(i) for i in np.cumsum(SPLIT_SIZES)[:-1]]
    for l in range(DEPTH):
        h = rms_norm(x, attn_norm_w[l])
        proj = h @ w_in[l]
        q, k, v, gate_b, gate_c, u = jnp.split(proj, split_idx, axis=-1)
        attn = banded_attention(q.reshape(bsz, s, N_Q_HEADS, HEAD_DIM),
                                k.reshape(bsz, s, N_KV_HEADS, HEAD_DIM),
                                v.reshape(bsz, s, N_KV_HEADS, HEAD_DIM),
                                sink_logits[l])
        conv = gate_b * dwconv3(gate_c * u, mix_conv_w[l], mix_conv_b[l])
        mixed = jnp.concatenate([rms_norm(attn, attn_out_norm_w[l]),
                                 rms_norm(conv, conv_out_norm_w[l])], axis=-1)
        x = x + mixed @ w_out[l]
        h = rms_norm(x, ffn_norm_w[l])
        g = dwconv3(h @ w_gate[l], ffn_conv_w[l], ffn_conv_b[l])
        x = x + (jax.nn.silu(g) * (h @ w_up[l])) @ w_down[l]
    return rms_norm(x, final_norm_w)
```

```python
import numpy as np
import concourse.bass as bass
import concourse.mybir as mybir
from concourse.bass_utils import run_bass_kernel_spmd

F32 = mybir.dt.float32
BF16 = mybir.dt.bfloat16
AF = mybir.ActivationFunctionType
ALU = mybir.AluOpType

D = 2048
S = 4096
NB = 2
TOK = 1024
HALO = 129
EXT = TOK + 2 * HALO
NQ = TOK + 2
DFF = 5632
NCH = DFF // 128
GRP = 11
NGRP = NCH // GRP
INW = 4608
EPS = 1e-6
NEG = -30000.0
NPAR = 232

SMALL = 0
A0 = 2560
XRES = A0
H2T = XRES + 65536
MIXT = H2T + 32832
REST = MIXT + 32832
ARENA = REST + 73728
H1T = A0
QT = H1T + 41024
KT = QT + 16416
VTOK = KT + 5128
SQ = VTOK + 5632
RSTDBC = SQ + 16416
assert RSTDBC + 8208 <= MIXT
WBUF = REST
WOB0 = REST + 16384
R2 = REST + 32768
PT = R2
CTMP = PT + 16896
BIAST = CTMP + 12360
assert BIAST + 6144 <= ARENA
XT = R2
XT4 = MIXT
HB = XT + 16384
JUNKA = HB + 12288
assert JUNKA + 4096 <= ARENA
W1BC = WOB0 + 8192
XHALO = R2
H2TMP = XHALO + 8192
JUNKE = H2TMP + 8192
W2BC = JUNKE + 4096
assert W2BC + 8192 <= ARENA
ACT0 = MIXT
WDB = ACT0 + 45056
FTMP = WDB + 22528
GUB = ARENA - 24576
assert FTMP + 8192 <= GUB
WFBC = GUB
JUNKG = GUB + 8192
SM_IDENT = 0
SM_ONES = 256
SM_PAR = 512
SM_KB = 1440
SM_ST = 1504
SM_HF = 2016
SM_EPS = 2024
SM_IDF = 2048


class H:
    __slots__ = ("id", "eng", "count", "dsem", "dval")

    def __init__(self, id, eng):
        self.id = id
        self.eng = eng
        self.count = None
        self.dsem = None
        self.dval = None


class Buf:
    ALL = []

    def __init__(self, name, off=None, size=0, group=None):
        self.name = name
        self.w = {}
        self.r = {}
        self.off = off
        self.size = size
        self.group = group
        self._al = None
        Buf.ALL.append(self)

    def aliases(self):
        if self._al is None:
            self._al = []
            if self.off is not None:
                for b in Buf.ALL:
                    if b is self or b.off is None:
                        continue
                    if self.group is not None and b.group == self.group:
                        continue
                    if b.off < self.off + self.size and self.off < b.off + b.size:
                        self._al.append(b)
        return self._al

    def wdeps(self, include_self=True):
        d = []
        if include_self:
            d += list(self.w.values()) + list(self.r.values())
        for a in self.aliases():
            d += list(a.w.values()) + list(a.r.values())
        return d

    def rdeps(self):
        return list(self.w.values())

    def wrote(self, h, keep=False):
        if not keep:
            self.w = {}
        self.r = {}
        self.w[h.eng if h.dsem is None else ("dma", id(h.dsem))] = h

    def read(self, h):
        self.r[h.eng] = h


class Sched:
    def __init__(self, nc, dry, signaled):
        self.nc = nc
        self.dry = dry
        self.signaled = signaled
        self.n = 0
        self.engs = {"pe": nc.tensor, "act": nc.scalar, "dve": nc.vector, "pool": nc.gpsimd, "sp": nc.sync}
        self.count = {e: 0 for e in self.engs}
        self.waited = {e: {} for e in self.engs}
        self.dslots = {}
        if not dry:
            self.sem = {e: nc.alloc_semaphore("s_" + e) for e in self.engs}

    def _wait(self, eng, deps):
        E = self.engs[eng]
        for d in deps:
            if d is None:
                continue
            if d.dsem is not None:
                key = ("dma", id(d.dsem))
                if self.waited[eng].get(key, 0) < d.dval:
                    E.wait_ge(d.dsem, d.dval)
                    self.waited[eng][key] = d.dval
            else:
                if d.eng == "pe" and eng == "pe":
                    continue
                if self.waited[eng].get(d.eng, 0) < d.count:
                    E.wait_ge(self.sem[d.eng], d.count)
                    self.waited[eng][d.eng] = d.count

    def op(self, eng, fn, deps=()):
        h = H(self.n, eng)
        self.n += 1
        if self.dry:
            for d in deps:
                if d is not None:
                    self.signaled.add(d.id)
            return h
        self._wait(eng, deps)
        inst = fn(self.engs[eng])
        if h.id in self.signaled:
            self.count[eng] += 1
            inst.then_inc(self.sem[eng], 1)
        h.count = self.count[eng]
        return h

    def fence(self, eng, deps):
        if self.dry:
            for d in deps:
                if d is not None and d.dsem is None:
                    self.signaled.add(d.id)
            return
        self._wait(eng, deps)

    def dma(self, queue, slot, out, in_, deps=()):
        h = H(self.n, queue)
        self.n += 1
        if slot not in self.dslots:
            self.dslots[slot] = [None if self.dry else self.nc.alloc_semaphore("d_" + slot), 0]
        st = self.dslots[slot]
        st[1] += 16
        h.dsem = st[0] if not self.dry else slot
        h.dval = st[1]
        if self.dry:
            for d in deps:
                if d is not None:
                    self.signaled.add(d.id)
            return h
        self._wait(queue, deps)
        self.engs[queue].dma_start(out=out, in_=in_).then_inc(st[0], 16)
        return h


def _ranges(c0, n):
    out = []
    off = 0
    while n > 0:
        m = min(512, n)
        out.append((c0, m, off))
        c0 += m
        off += m
        n -= m
    return out


def build_program():
    nc = bass.Bass("TRN2", target_bir_lowering=False)
    dt = {}
    dt["x_ext"] = nc.dram_tensor("x_ext", [EXT, D], F32, kind="ExternalInput").ap()
    dt["w_in"] = nc.dram_tensor("w_in", [D, INW], F32, kind="ExternalInput").ap()
    dt["w_out"] = nc.dram_tensor("w_out", [D, D], F32, kind="ExternalInput").ap()
    dt["w_gate"] = nc.dram_tensor("w_gate", [D, DFF], F32, kind="ExternalInput").ap()
    dt["w_up"] = nc.dram_tensor("w_up", [D, DFF], F32, kind="ExternalInput").ap()
    dt["w_down"] = nc.dram_tensor("w_down", [DFF, D], F32, kind="ExternalInput").ap()
    dt["attn_norm_w"] = nc.dram_tensor("attn_norm_w", [D], F32, kind="ExternalInput").ap()
    dt["ffn_norm_w"] = nc.dram_tensor("ffn_norm_w", [D], F32, kind="ExternalInput").ap()
    dt["final_norm_w"] = nc.dram_tensor("final_norm_w", [D], F32, kind="ExternalInput").ap()
    dt["params"] = nc.dram_tensor("params", [128, NPAR], F32, kind="ExternalInput").ap()
    dt["kbias"] = nc.dram_tensor("kbias", [128, 11], F32, kind="ExternalInput").ap()
    dt["hflag"] = nc.dram_tensor("hflag", [2, 1], F32, kind="ExternalInput").ap()
    dt["biast"] = nc.dram_tensor("biast", [128, 8 * 384], F32, kind="ExternalInput").ap()
    dt["ident"] = nc.dram_tensor("ident", [128, 128], F32, kind="ExternalInput").ap()
    dt["y"] = nc.dram_tensor("y", [TOK, D], F32, kind="ExternalOutput").ap()

    arena = nc.alloc_sbuf_tensor("arena", [128, ARENA // 4], F32)
    ps = nc.alloc_psum_tensor("ps", [128, 4096], F32)

    signaled = set()
    emit(nc, dt, arena, ps, Sched(nc, True, signaled))
    emit(nc, dt, arena, ps, Sched(nc, False, signaled))
    return nc


def emit(nc, dt, arena, ps, S_):
    op, dma = S_.op, S_.dma
    Buf.ALL = []

    def cv(off, shape, dtype):
        n = int(np.prod(shape))
        sz = 2 if dtype == BF16 else 4
        assert off % 4 == 0
        w = (n * sz + 3) // 4
        assert off + 4 * w <= ARENA, (off, shape)
        ap = arena[:, off // 4: off // 4 + w]
        if dtype == BF16:
            ap = ap.bitcast(BF16)[:, 0:n]
        if len(shape) == 2:
            ap = ap.rearrange("p (a b) -> p a b", a=shape[0], b=shape[1])
        elif len(shape) == 3:
            ap = ap.rearrange("p (a b c) -> p a b c", a=shape[0], b=shape[1], c=shape[2])
        return ap

    def psb(bank):
        return ps[:, bank * 512:(bank + 1) * 512].bitcast(BF16).rearrange("p (a b) -> p a b", a=8, b=128)

    ident = cv(SM_IDENT, [128], BF16)
    ones = cv(SM_ONES, [128], BF16)
    par = cv(SM_PAR, [NPAR], F32)
    kb = cv(SM_KB, [11], F32)
    st = cv(SM_ST, [128], F32)
    hf = cv(SM_HF, [1], F32)
    epsT = cv(SM_EPS, [1], F32)
    P_AW, P_CW, P_MW0, P_MW1, P_MW2, P_MB, P_SINK, P_FW0 = 0, 8, 16, 24, 32, 40, 48, 56
    P_FW1, P_FW2, P_FB = P_FW0 + 44, P_FW0 + 88, P_FW0 + 132
    ST_SSQ1, ST_LN1, ST_R1 = 0, 11, 22
    ST_ESINK = 33
    ST_SSQ2, ST_LN2, ST_R2 = 41, 50, 59
    ST_SSQ3, ST_LN3, ST_R3 = 68, 76, 84

    h1T = cv(H1T, [16, EXT], BF16)
    qT = cv(QT, [8, NQ], BF16)
    kT = cv(KT, [2, EXT], BF16)
    vtok = cv(VTOK, [11, 256], BF16)
    sq = cv(SQ, [8, NQ], BF16)
    rstdbc = cv(RSTDBC, [2, NQ], F32)
    mixT = cv(MIXT, [16, NQ], BF16)
    wbuf = cv(WBUF, [6, 16, 128], BF16)
    wob = [cv(WOB0, [16, 512], BF16), cv(WBUF, [16, 512], BF16)]
    PTb = [cv(PT, [11, 384], BF16), cv(PT + 8448, [11, 384], BF16)]
    u_sb = cv(CTMP, [1032], F32)
    cu = cv(CTMP + 4128, [1032], F32)
    ycv = cv(CTMP + 8256, [1026], F32)
    biasT = cv(BIAST, [8, 384], BF16)
    xt = [cv(XT, [2048], F32), cv(XT + 8192, [2048], F32)]
    NXA = 8
    xa_off = [XT4 + i * 8192 for i in range(4)] + [QT, QT + 8192, SQ, SQ + 8192]
    xa = [cv(o, [2048], F32) for o in xa_off]
    hb = [cv(HB + i * 4096, [2048], BF16) for i in range(3)]
    junkA = cv(JUNKA, [2048], BF16)
    w1bc = cv(W1BC, [2048], F32)
    xres = cv(XRES, [8, 2048], F32)
    h2T = cv(H2T, [16, NQ], BF16)
    xhalo = cv(XHALO, [2048], F32)
    h2tmp = [cv(H2TMP, [2048], BF16), cv(H2TMP + 4096, [2048], BF16)]
    junkE = cv(JUNKE, [2048], BF16)
    w2bc = cv(W2BC, [2048], F32)
    actb = [cv(ACT0, [GRP, 1024], BF16), cv(ACT0 + 22528, [GRP, 1024], BF16)]
    gub = cv(GUB, [6, 16, 128], BF16)
    wdb = [cv(WDB, [GRP, 512], BF16), cv(WDB + 11264, [GRP, 512], BF16)]
    ftmp = [cv(FTMP, [1024], F32), cv(FTMP + 4096, [1024], F32)]
    wfbc = cv(WFBC, [2048], F32)
    junkG = cv(JUNKG, [2048], BF16)
    wdx = [cv(ACT0, [GRP, 512], BF16), cv(ACT0 + 11264, [GRP, 512], BF16)]

    bank = [Buf(f"bank{i}") for i in range(8)]
    b_small = Buf("small")
    b_h1T = [Buf(f"h1T{t}", H1T, 41024, "h1T") for t in range(11)]
    b_qT = [Buf(f"qT{h}", QT + h * NQ * 2, NQ * 2) for h in range(8)]
    b_kT = Buf("kT", KT, 5128)
    b_vtok = Buf("vtok", VTOK, 5632)
    b_wbuf = [Buf(f"wbuf{i}", WBUF + i * 4096, 4096) for i in range(6)]
    b_w1bc = Buf("w1bc", W1BC, 8192)
    b_biasT = Buf("biasT", BIAST, 6144)
    b_xt = [Buf(f"xt{i}", XT + i * 8192, 8192) for i in range(2)]
    b_xa = [Buf(f"xa{i}", xa_off[i], 8192) for i in range(NXA)]
    b_hb = [Buf(f"hb{i}", HB + i * 4096, 4096) for i in range(3)]
    b_junkA = Buf("junkA", JUNKA, 4096)
    b_PT = [Buf(f"PT{i}", PT + i * 8448, 8448) for i in range(2)]
    b_usb = Buf("u_sb", CTMP, 4128)
    b_cu = Buf("cu", CTMP + 4128, 4128)
    b_y = Buf("ycv", CTMP + 8256, 4104)
    b_usbu = [Buf(f"usb_u{u}", CTMP + 4 * 384 * u, 4 * 384) for u in range(3)]
    b_cuu = [Buf(f"cu_u{u}", CTMP + 4128 + 4 * 384 * u, 4 * 384) for u in range(3)]
    b_sq = Buf("sq", SQ, 16416)
    b_rbc = Buf("rstdbc", RSTDBC, 8208)
    b_mix = [Buf(f"mix{c}", MIXT + c * NQ * 2, NQ * 2) for c in range(16)]
    b_xres = [Buf(f"xres{i}", XRES + i * 8192, 8192) for i in range(8)]
    b_h2T = [Buf(f"h2T{t}", H2T, 32832, "h2T") for t in range(9)]
    b_wob = [Buf("wob0", WOB0, 16384), Buf("wob1", WBUF, 16384)]
    b_xhalo = Buf("xhalo", XHALO, 8192)
    b_h2tmp = [Buf(f"h2tmp{i}", H2TMP + i * 4096, 4096) for i in range(2)]
    b_junkE = Buf("junkE", JUNKE, 4096)
    b_w2bc = Buf("w2bc", W2BC, 8192)
    b_act = [[Buf(f"act{g}_{c}", ACT0 + g * 22528 + c * 2048, 2048) for c in range(GRP)] for g in range(2)]
    b_gub = [Buf(f"gub{i}", GUB + i * 4096, 4096) for i in range(6)]
    b_wdb = [Buf(f"wdb{i}", WDB + i * 11264, 11264) for i in range(2)]
    b_ftmp = [Buf(f"ftmp{i}", FTMP + i * 4096, 4096) for i in range(2)]
    b_wfbc = Buf("wfbc", WFBC, 8192)
    b_wdx = [Buf(f"wdx{i}", ACT0 + i * 11264, 11264) for i in range(2)]
    b_junkG = Buf("junkG", JUNKG, 4096)

    def W(bufs):
        d = []
        for b in bufs:
            d += b.wdeps()
        return d

    def R(bufs):
        d = []
        for b in bufs:
            d += b.rdeps()
        return d

    def h1_tiles(c0, n):
        ts = [t for t in range(10) if 128 * t < c0 + n and c0 < 128 * t + 128]
        if c0 + n > 1280:
            ts.append(10)
        return [b_h1T[t] for t in ts]

    hx0 = dma("sp", "xa0", cv(XT4, [2048], F32), dt["x_ext"][0:128, :])
    h = dma("sp", "par", par, dt["params"][:, :])
    b_small.wrote(h, keep=True)
    h = dma("sp", "kb", kb, dt["kbias"][:, :])
    b_small.wrote(h, keep=True)
    h = dma("sp", "hf", hf[0:2, :], dt["hflag"][:, :])
    b_small.wrote(h, keep=True)
    h_w1 = dma("sp", "w1bc", w1bc, dt["attn_norm_w"].partition_broadcast(128))
    b_w1bc.wrote(h_w1)
    h_id = dma("sp", "idl", xt[1][:, 0:128], dt["ident"][:, :])
    b_xt[1].wrote(h_id)
    h = op("dve", lambda e: e.tensor_copy(out=ident, in_=xt[1][:, 0:128]), R([b_xt[1]]))
    b_small.wrote(h, keep=True); b_xt[1].read(h)
    h = op("dve", lambda e: e.memset(ones, 1.0))
    b_small.wrote(h, keep=True)
    h = op("dve", lambda e: e.memset(st, 0.0))
    b_small.wrote(h, keep=True)
    h = op("dve", lambda e: e.memset(epsT, EPS))
    b_small.wrote(h, keep=True)
    h = op("act", lambda e: e.activation(out=st[:, ST_ESINK:ST_ESINK + 8], in_=par[:, P_SINK:P_SINK + 8], func=AF.Exp),
           R([b_small]))
    b_small.wrote(h, keep=True)

    w_in_v = dt["w_in"].rearrange("(kc p) n -> p kc n", p=128)
    blocks = [("v", 0, 1280), ("v", 1, 1408), ("k", 0, 1024), ("k", 1, 1152)]
    blocks += [("q", hh, 128 * hh) for hh in range(8)]
    for g in range(8):
        blocks += [("u", g, 3584 + 128 * g), ("c", g, 2560 + 128 * g), ("b", g, 1536 + 128 * g)]
    wload = {}
    nissued = [0]

    def wslot(i):
        return i if i < 6 else (i - 6) % 4

    def issue_w(upto, extra=()):
        while nissued[0] < min(upto, len(blocks)):
            i = nissued[0]
            s = wslot(i)
            col = blocks[i][2]
            hh = dma("pool", f"wb{s}", wbuf[:, s], w_in_v[:, :, col:col + 128], W([b_wbuf[s]]) + list(extra))
            b_wbuf[s].wrote(hh)
            wload[i] = hh
            nissued[0] += 1

    tilesA = [(128 * i, 128) for i in range(10)] + [(1280, 2)]

    hxa = {}

    def a_dma(t, extra=()):
        r0, nr = tilesA[t]
        bi = t % NXA
        if t == 0:
            hx = hx0
        else:
            hx = dma("sp", f"xa{bi}", xa[bi][0:nr, :], dt["x_ext"][r0:r0 + nr, :], W([b_xa[bi]]) + list(extra))
        b_xa[bi].wrote(hx)
        hxa[t] = hx

    def a_s1(t):
        r0, nr = tilesA[t]
        bi = t % NXA
        h = op("act", lambda e: e.activation(
            out=junkA[0:nr, :], in_=xa[bi][0:nr, :], func=AF.Square, accum_out=st[0:nr, ST_SSQ1 + t:ST_SSQ1 + t + 1]),
            R([b_xa[bi], b_small]) + b_junkA.wdeps(False))
        b_xa[bi].read(h); b_junkA.wrote(h, keep=True)
        h = op("act", lambda e: e.activation(
            out=st[0:nr, ST_LN1 + t:ST_LN1 + t + 1], in_=st[0:nr, ST_SSQ1 + t:ST_SSQ1 + t + 1], func=AF.Ln,
            scale=1.0 / D, bias=epsT[0:nr, :]), [h])
        h = op("act", lambda e: e.activation(
            out=st[0:nr, ST_R1 + t:ST_R1 + t + 1], in_=st[0:nr, ST_LN1 + t:ST_LN1 + t + 1], func=AF.Exp, scale=-0.5), [h])
        return h

    def a_s2(t, hr):
        r0, nr = tilesA[t]
        bi = t % 3
        xi = t % NXA
        h = op("dve", lambda e: e.scalar_tensor_tensor(
            out=hb[bi][0:nr, :], in0=xa[xi][0:nr, :], scalar=st[0:nr, ST_R1 + t:ST_R1 + t + 1], in1=w1bc[0:nr, :],
            op0=ALU.mult, op1=ALU.mult), [hr] + R([b_xa[xi], b_w1bc]) + W([b_hb[bi]]))
        b_xa[xi].read(h); b_hb[bi].wrote(h); b_w1bc.read(h)

    def a_s3(t):
        r0, nr = tilesA[t]
        bi = t % 2
        hi = t % 3
        bk = [2 * bi, 2 * bi + 1]
        deps = R([b_hb[hi], b_small]) + W([bank[bk[0]], bank[bk[1]]])
        for kc in range(16):
            h = op("pe", lambda e, kc=kc: e.transpose(
                out=psb(bk[kc // 8])[:, kc % 8, 0:nr], in_=hb[hi][0:nr, kc * 128:(kc + 1) * 128],
                identity=ident[0:nr, 0:nr]), deps if kc == 0 else ())
        b_hb[hi].read(h); bank[bk[0]].wrote(h); bank[bk[1]].wrote(h)

    def a_s4(t):
        r0, nr = tilesA[t]
        bi = t % 2
        bk = [2 * bi, 2 * bi + 1]
        h = op("act", lambda e: e.activation(
            out=h1T[:, 0:8, r0:r0 + nr], in_=psb(bk[0])[:, :, 0:nr], func=AF.Copy), R([bank[bk[0]]]))
        bank[bk[0]].read(h); b_h1T[t].wrote(h, keep=True)
        h = op("dve", lambda e: e.tensor_copy(
            out=h1T[:, 8:16, r0:r0 + nr], in_=psb(bk[1])[:, :, 0:nr]), R([bank[bk[1]]]))
        bank[bk[1]].read(h); b_h1T[t].wrote(h, keep=True)

    kcount = [0]

    def k_range(g, rg):
        c0, m, _ = rg
        bk = 4 + (kcount[0] % 2)
        kcount[0] += 1
        deps = [wload[2 + g]] + R(h1_tiles(c0, m)) + W([bank[bk]])
        for kc in range(16):
            h = op("pe", lambda e, kc=kc: e.matmul(
                ps[:, bk * 512: bk * 512 + m], lhsT=wbuf[:, 2 + g, kc, :], rhs=h1T[:, kc, c0:c0 + m],
                start=(kc == 0), stop=(kc == 15)), deps if kc == 0 else ())
        bank[bk].wrote(h); b_wbuf[2 + g].read(h)
        for bb in h1_tiles(c0, m):
            bb.read(h)
        if kcount[0] % 2 == 0:
            h2 = op("act", lambda e: e.activation(out=kT[:, g, c0:c0 + m], in_=ps[:, bk * 512: bk * 512 + m], func=AF.Copy),
                    R([bank[bk]]) + b_kT.wdeps(False))
        else:
            h2 = op("dve", lambda e: e.tensor_copy(out=kT[:, g, c0:c0 + m], in_=ps[:, bk * 512: bk * 512 + m]),
                    R([bank[bk]]) + b_kT.wdeps(False))
        bank[bk].read(h2); b_kT.wrote(h2, keep=True)

    def v_block(j):
        nk = 128 if j < 10 else 2
        vb = 6 + (j % 2)
        deps = [wload[0], wload[1]] + R([b_h1T[j]]) + W([bank[vb]])
        for kc in range(16):
            h = op("pe", lambda e, kc=kc: e.matmul(
                ps[0:nk, vb * 512: vb * 512 + 256], lhsT=h1T[:, kc, 128 * j:128 * j + nk], rhs=wbuf[:, 0:2, kc, :],
                start=(kc == 0), stop=(kc == 15)), deps if kc == 0 else ())
        bank[vb].wrote(h); b_h1T[j].read(h); b_wbuf[0].read(h); b_wbuf[1].read(h)
        if j % 2 == 0:
            h2 = op("dve", lambda e: e.tensor_copy(out=vtok[0:nk, j, :], in_=ps[0:nk, vb * 512: vb * 512 + 256]),
                    R([bank[vb]]) + b_vtok.wdeps(False))
        else:
            h2 = op("act", lambda e: e.activation(out=vtok[0:nk, j, :], in_=ps[0:nk, vb * 512: vb * 512 + 256], func=AF.Copy),
                    R([bank[vb]]) + b_vtok.wdeps(False))
        bank[vb].read(h2); b_vtok.wrote(h2, keep=True)

    krg = _ranges(0, EXT)
    hrs = {}
    a_dma(0)
    issue_w(2)
    a_dma(1)
    a_dma(2)
    for t in range(3, NXA):
        a_dma(t, extra=[] if t == 3 else ([hxa[t - 2]] if t >= 6 else [wload[1]]))
    issue_w(4, extra=[hxa[3]])
    hrs[0] = a_s1(0)
    hrs[1] = a_s1(1)
    a_s2(0, hrs[0])
    VD = 2
    for t in range(11):
        if t >= 1:
            a_s4(t - 1)
        if t + 2 < 11:
            hrs[t + 2] = a_s1(t + 2)
        if t + 1 < 11:
            a_s2(t + 1, hrs[t + 1])
        if t + NXA < 11:
            a_dma(t + NXA, extra=[hxa[t + NXA - 2]])
        if t == 3:
            issue_w(6, extra=[hxa[10]])
        a_s3(t)
        if t >= VD:
            v_block(t - VD)
        if t == 5:
            k_range(0, krg[0]); k_range(1, krg[0])
        if t == 9:
            k_range(0, krg[1]); k_range(1, krg[1])
    a_s4(10)
    for j in range(11 - VD, 11):
        v_block(j)
    k_range(0, krg[2]); k_range(1, krg[2])
    bi_ = 4
    issue_w(10)

    for half in range(2):
        hh = dma("sp", "bt", xt[0][:, 0:1536], dt["biast"][:, half * 1536:(half + 1) * 1536], W([b_xt[0]]) + [hxa[10]])
        b_xt[0].wrote(hh)
        h = op("dve", lambda e, half=half: e.tensor_copy(
            out=biasT[:, half * 4:(half + 1) * 4, :].rearrange("p a b -> p (a b)"), in_=xt[0][:, 0:1536]),
            R([b_xt[0]]) + b_biasT.wdeps(False))
        b_xt[0].read(h); b_biasT.wrote(h, keep=True)

    def win_block(i, set_, c0, n, b0=None):
        s = wslot(i)
        w3 = (n + 2) // 3
        rg = [(c0 + r * w3, w3, r * 512) for r in range(3)]
        if b0 is None:
            b0 = 3 * set_
        bks = [bank[b0 + k] for k in range(3)]
        deps = [wload[i]] + R(b_h1T) + W(bks)
        first = True
        for kc in range(16):
            for (cc, m, off) in rg:
                h = op("pe", lambda e, kc=kc, cc=cc, m=m, off=off: e.matmul(
                    ps[:, b0 * 512 + off: b0 * 512 + off + m], lhsT=wbuf[:, s, kc, :], rhs=h1T[:, kc, cc:cc + m],
                    start=(kc == 0), stop=(kc == 15)), deps if first else ())
                first = False
        for bb in bks:
            bb.wrote(h)
        b_wbuf[s].read(h)
        for bb in b_h1T:
            bb.read(h)
        issue_w(i + 5)
        return h, bks

    def ps3(b0, w3):
        return ps[:, b0 * 512:(b0 + 3) * 512].rearrange("p (a b) -> p a b", a=3, b=512)[:, :, 0:w3]

    def v3(ap, w3):
        return ap[:, 0:3 * w3].rearrange("p (a b) -> p a b", a=3, b=w3)

    set_ = 0
    for hq in range(8):
        h, bks = win_block(bi_, set_, 128, NQ)
        h = op("act", lambda e, hq=hq, set_=set_: e.activation(
            out=v3(qT[:, hq, :], 342), in_=ps3(3 * set_, 342), func=AF.Copy, scale=float(128 ** -0.5)),
            R(bks) + W([b_qT[hq]]))
        for bb in bks:
            bb.read(h)
        b_qT[hq].wrote(h)
        bi_ += 1; set_ ^= 1

    scount = [0]
    ucount = [0]

    pend = {}

    def att_S(hq, j):
        g = hq // 4
        pb = hq % 2
        PTh = PTb[pb]
        nk = 128 if j < 10 else 2
        qlo = max(128, 128 * (j - 1)); qhi = min(128 + NQ, 128 * (j + 2))
        nq = qhi - qlo
        tc0 = qlo - 128 * (j - 1)
        paired = 2 <= j <= 7
        if paired and j % 2 == 0 and scount[0] % 2 == 1:
            scount[0] += 1
        sb = 4 + (scount[0] % 4)
        scount[0] += 1
        deps = R([b_kT, b_qT[hq], b_biasT, b_small]) + W([bank[sb]])
        op("pe", lambda e: e.matmul(
            ps[0:nk, sb * 512: sb * 512 + nq], lhsT=kT[:, g, 128 * j:128 * j + nk], rhs=qT[:, hq, qlo - 128:qlo - 128 + nq],
            start=True, stop=False), deps)
        h = op("pe", lambda e: e.matmul(
            ps[0:nk, sb * 512: sb * 512 + nq], lhsT=ident[:, 0:nk], rhs=biasT[:, hq, tc0:tc0 + nq],
            start=False, stop=True))
        bank[sb].wrote(h)
        b_kT.read(h); b_qT[hq].read(h); b_biasT.read(h)
        if paired and j % 2 == 0:
            pend[hq] = sb
            return
        if paired:
            s0 = pend.pop(hq)
            assert sb == s0 + 1 and nq == 384 and tc0 == 0
            h = op("act", lambda e: e.activation(
                out=PTh[:, j - 1:j + 1, 0:384],
                in_=ps[:, s0 * 512:(s0 + 2) * 512].rearrange("p (a b) -> p a b", a=2, b=512)[:, :, 0:384],
                func=AF.Exp), R([bank[s0], bank[sb]]))
            bank[s0].read(h); bank[sb].read(h); b_PT[pb].wrote(h, keep=True)
            return
        h = op("act", lambda e: e.activation(
            out=PTh[0:nk, j, tc0:tc0 + nq], in_=ps[0:nk, sb * 512: sb * 512 + nq], func=AF.Exp, bias=kb[0:nk, j:j + 1]),
            (R([bank[sb]]) + W([b_PT[pb]])) if j == 0 else R([bank[sb]]))
        bank[sb].read(h); b_PT[pb].wrote(h, keep=True)

    def att_QB(hq, i, first_of_unit, pvb, dnb, u):
        g = hq // 4
        pb = hq % 2
        PTh = PTb[pb]
        nqb = 128 if i < 9 else 2
        js = [j for j in (i - 1, i, i + 1) if 0 <= j <= 10]
        o = 128 * (i - 1) - 384 * u
        deps = R([b_PT[pb], b_vtok, b_small]) + (W([bank[pvb], bank[dnb]]) if first_of_unit else [])
        first = True
        for which in range(2):
            bk = pvb if which == 0 else dnb
            for jj, j in enumerate(js):
                nk = 128 if j < 10 else 2
                tc = 128 * (i - j + 1)
                lhs = vtok[0:nk, j, g * 128:(g + 1) * 128] if which == 0 else ones[0:nk, :]
                h = op("pe", lambda e: e.matmul(
                    ps[:, bk * 512 + o: bk * 512 + o + nqb], lhsT=lhs, rhs=PTh[0:nk, j, tc:tc + nqb],
                    start=(jj == 0), stop=(jj == len(js) - 1)), deps if first else ())
                first = False
        bank[pvb].wrote(h); bank[dnb].wrote(h)
        b_PT[pb].read(h); b_vtok.read(h)

    def att_C(hq, u, pvb, dnb):
        n = 384 if u < 2 else 258
        c0 = 384 * u
        rr = u_sb[:, c0:c0 + n]
        aa = cu[:, c0:c0 + n]
        h = op("act", lambda e: e.activation(out=rr, in_=ps[:, dnb * 512: dnb * 512 + n], func=AF.Ln,
                                             bias=st[:, ST_ESINK + hq:ST_ESINK + hq + 1]),
               R([bank[dnb], b_small]) + W([b_usbu[u]]))
        b_usbu[u].wrote(h); bank[dnb].read(h)
        h = op("act", lambda e: e.activation(out=rr, in_=rr, func=AF.Exp, scale=-1.0), [h])
        b_usbu[u].wrote(h)
        h = op("dve", lambda e: e.tensor_tensor(out=aa, in0=ps[:, pvb * 512: pvb * 512 + n], in1=rr, op=ALU.mult),
               [h] + R([bank[pvb]]) + W([b_cuu[u]]))
        bank[pvb].read(h); b_usbu[u].read(h); b_cuu[u].wrote(h)
        h1 = op("dve", lambda e: e.tensor_scalar_mul(out=mixT[:, hq, c0:c0 + n], in0=aa, scalar1=par[:, P_AW + hq:P_AW + hq + 1]),
                [h] + R([b_small]) + (W([b_mix[hq]]) if u == 0 else b_mix[hq].wdeps(False)))
        b_mix[hq].wrote(h1, keep=True); b_cuu[u].read(h1)
        h2 = op("dve", lambda e: e.tensor_tensor(out=sq[:, hq, c0:c0 + n], in0=aa, in1=aa, op=ALU.mult),
                [h] + b_sq.wdeps(False))
        b_sq.wrote(h2, keep=True); b_cuu[u].read(h2)

    hxr = {}

    def load_xres(i):
        hx = dma("sp", f"xr{i}", xres[:, i, :], dt["x_ext"][129 + 128 * i:257 + 128 * i, :], W([b_xres[i]]))
        b_xres[i].wrote(hx)
        hxr[i] = hx

    for j in range(11):
        att_S(0, j)
    for hq in range(8):
        nxt = list(range(11)) if hq + 1 < 8 else []
        for i in range(1, 10):
            u = min((i - 1) // 3, 2)
            first_of_unit = (i - 1) % 3 == 0 and i < 9
            if first_of_unit:
                pvb, dnb = (0, 1) if ucount[0] % 2 == 0 else (2, 3)
                ucount[0] += 1
            for _ in range(2 if i in (1, 5) else 1):
                if nxt:
                    att_S(hq + 1, nxt.pop(0))
            att_QB(hq, i, first_of_unit, pvb, dnb, u)
            if i in (3, 6, 9):
                att_C(hq, u, pvb, dnb)

    load_xres(6)
    load_xres(7)

    def stats_part(half, b0=0):
        rg = _ranges(0, NQ)
        bks = [bank[b0], bank[b0 + 1], bank[b0 + 2]]
        deps = R([b_sq, b_small]) + W(bks)
        first = True
        for c in range(8):
            for (cc, m, off) in rg:
                h = op("pe", lambda e, c=c, cc=cc, m=m, off=off: e.matmul(
                    ps[:, b0 * 512 + off:b0 * 512 + off + m], lhsT=ones, rhs=sq[:, c, cc:cc + m], start=(c == 0), stop=(c == 7)),
                    deps if first else ())
                first = False
        for bb in bks:
            bb.wrote(h)
        b_sq.read(h)
        h = op("act", lambda e: e.activation(out=rstdbc[:, half, :], in_=ps[:, b0 * 512:b0 * 512 + NQ], func=AF.Ln, scale=1.0 / 1024, bias=epsT),
               R(bks + [b_small]) + W([b_rbc]))
        for bb in bks:
            bb.read(h)
        b_rbc.wrote(h, keep=True)
        h = op("act", lambda e: e.activation(out=rstdbc[:, half, :], in_=rstdbc[:, half, :], func=AF.Exp, scale=-0.5), [h])
        b_rbc.wrote(h, keep=True)
        return h

    def normalize_part(half, h):
        for c in range(8):
            cc = half * 8 + c
            h2 = op("dve", lambda e, cc=cc: e.tensor_tensor(out=mixT[:, cc, :], in0=mixT[:, cc, :], in1=rstdbc[:, half, :], op=ALU.mult),
                    [h] + R([b_mix[cc]]))
            b_mix[cc].wrote(h2); b_rbc.read(h2)


    w_out_v = dt["w_out"].rearrange("(kc p) n -> p kc n", p=128)
    wo_load = {}

    def issue_wo(n, extra=()):
        hh = dma("pool", f"wo{n % 2}", wob[n % 2], w_out_v[:, :, n * 512:(n + 1) * 512], W([b_wob[n % 2]]) + list(extra))
        b_wob[n % 2].wrote(hh)
        wo_load[n] = hh

    for g in range(8):
        ub0 = 4 if g == 0 else 3 * set_
        h, bks = win_block(bi_, set_, 127, 1028, b0=ub0)
        h = op("act", lambda e, ub0=ub0: e.activation(out=v3(u_sb, 343), in_=ps3(ub0, 343), func=AF.Copy),
               R(bks) + W([b_usb]))
        for bb in bks:
            bb.read(h)
        b_usb.wrote(h)
        bi_ += 1; set_ ^= 1
        if g == 0:
            normalize_part(0, stats_part(0, 3 * set_))
        h, bks = win_block(bi_, set_, 127, 1028)
        h = op("dve", lambda e, set_=set_: e.tensor_tensor(out=v3(cu, 343), in0=ps3(3 * set_, 343), in1=v3(u_sb, 343), op=ALU.mult),
               R(bks + [b_usb]) + W([b_cu]))
        for bb in bks:
            bb.read(h)
        b_usb.read(h); b_cu.wrote(h)
        bi_ += 1; set_ ^= 1
        h = op("act", lambda e, g=g: e.activation(out=ycv, in_=cu[:, 1:1027], func=AF.Identity,
                                                  scale=par[:, P_MW1 + g:P_MW1 + g + 1], bias=par[:, P_MB + g:P_MB + g + 1]),
               R([b_cu, b_small]) + W([b_y]))
        b_cu.read(h); b_y.wrote(h)
        h = op("dve", lambda e, g=g: e.scalar_tensor_tensor(out=ycv, in0=cu[:, 0:1026], scalar=par[:, P_MW0 + g:P_MW0 + g + 1], in1=ycv,
                                                            op0=ALU.mult, op1=ALU.add), R([b_y, b_cu]))
        b_y.wrote(h); b_cu.read(h)
        h = op("dve", lambda e, g=g: e.scalar_tensor_tensor(out=ycv, in0=cu[:, 2:1028], scalar=par[:, P_MW2 + g:P_MW2 + g + 1], in1=ycv,
                                                            op0=ALU.mult, op1=ALU.add), [h])
        b_y.wrote(h); b_cu.read(h)
        hy = h
        h, bks = win_block(bi_, set_, 128, NQ)
        h = op("dve", lambda e, set_=set_: e.tensor_tensor(out=v3(u_sb, 342), in0=ps3(3 * set_, 342), in1=v3(ycv, 342), op=ALU.mult),
               [hy] + R(bks) + W([b_usb]))
        for bb in bks:
            bb.read(h)
        b_usb.wrote(h); b_y.read(h)
        bi_ += 1; set_ ^= 1
        h1 = op("act", lambda e, g=g: e.activation(out=mixT[:, 8 + g, :], in_=u_sb[:, 0:NQ], func=AF.Copy, scale=par[:, P_CW + g:P_CW + g + 1]),
                R([b_usb, b_small]) + W([b_mix[8 + g]]))
        b_mix[8 + g].wrote(h1); b_usb.read(h1)
        h2 = op("act", lambda e, g=g: e.activation(out=sq[:, g, :], in_=u_sb[:, 0:NQ], func=AF.Square),
                (R([b_usb]) + W([b_sq])) if g == 0 else (R([b_usb]) + b_sq.wdeps(False)))
        b_sq.wrote(h2, keep=True); b_usb.read(h2)
        if g == 5:
            issue_wo(0)

    for i in range(6):
        load_xres(i)
    issue_wo(1, extra=[hxr[3]])
    hw2 = dma("sp", "w2bc", w2bc, dt["ffn_norm_w"].partition_broadcast(128), W([b_w2bc]))
    b_w2bc.wrote(hw2)
    hx = dma("sp", "xh", xhalo[0:2, :], dt["x_ext"][128:1154:1025, :], W([b_xhalo]))
    b_xhalo.wrote(hx)

    w_gate_v = dt["w_gate"].rearrange("(kc p) n -> p kc n", p=128)
    w_up_v = dt["w_up"].rearrange("(kc p) n -> p kc n", p=128)
    gu_load = {}
    gu_issued = [0]

    def gslot(c, k):
        return (2 * c + k + 3) % 6

    def issue_gu(upto_halves):
        while gu_issued[0] < min(upto_halves, 2 * NCH):
            c, k = gu_issued[0] // 2, gu_issued[0] % 2
            src = w_gate_v if k == 0 else w_up_v
            s = gslot(c, k)
            hh = dma("pool", f"gu{s}", gub[:, s], src[:, :, c * 128:(c + 1) * 128], W([b_gub[s]]))
            b_gub[s].wrote(hh)
            gu_load[(c, k)] = hh
            gu_issued[0] += 1

    def tile_cols(ti):
        if ti < 8:
            return slice(1 + 128 * ti, 129 + 128 * ti), 128
        return slice(0, 1026, 1025), 2

    n2_hr = {}

    def n2_stats(ti):
        cols, m = tile_cols(ti)
        bx = b_xres[ti] if ti < 8 else b_xhalo
        xfull = xres[:, ti, :] if ti < 8 else xhalo[0:2, :]
        hs = op("act", lambda e: e.activation(
            out=junkE[0:m, :], in_=xfull, func=AF.Square, accum_out=st[0:m, ST_SSQ2 + ti:ST_SSQ2 + ti + 1]),
            R([bx, b_small]) + b_junkE.wdeps(False))
        b_junkE.wrote(hs, keep=True); bx.read(hs)
        hl = op("act", lambda e: e.activation(
            out=st[0:m, ST_LN2 + ti:ST_LN2 + ti + 1], in_=st[0:m, ST_SSQ2 + ti:ST_SSQ2 + ti + 1], func=AF.Ln,
            scale=1.0 / D, bias=epsT[0:m, :]), [hs])
        n2_hr[ti] = op("act", lambda e: e.activation(
            out=st[0:m, ST_R2 + ti:ST_R2 + ti + 1], in_=st[0:m, ST_LN2 + ti:ST_LN2 + ti + 1], func=AF.Exp, scale=-0.5), [hl])

    def n2_scale(ti):
        cols, m = tile_cols(ti)
        bx = b_xres[ti] if ti < 8 else b_xhalo
        xfull = xres[:, ti, :] if ti < 8 else xhalo[0:2, :]
        hbi = ti % 2
        hr = n2_hr[ti]
        if ti == 8:
            hr = op("dve", lambda e: e.tensor_tensor(out=st[0:2, ST_R2 + 8:ST_R2 + 9], in0=st[0:2, ST_R2 + 8:ST_R2 + 9], in1=hf[0:2, :], op=ALU.mult),
                    [hr] + R([b_small]))
        hh = op("dve", lambda e: e.scalar_tensor_tensor(
            out=h2tmp[hbi][0:m, :], in0=xfull, scalar=st[0:m, ST_R2 + ti:ST_R2 + ti + 1], in1=w2bc[0:m, :],
            op0=ALU.mult, op1=ALU.mult), [hr] + R([bx, b_w2bc]) + W([b_h2tmp[hbi]]))
        b_h2tmp[hbi].wrote(hh); bx.read(hh); b_w2bc.read(hh)

    def n2_transpose(ti):
        cols, m = tile_cols(ti)
        hbi = ti % 2
        tb = [4 + 2 * hbi, 5 + 2 * hbi]
        deps = R([b_h2tmp[hbi], b_small]) + W([bank[tb[0]], bank[tb[1]]])
        for kc in range(16):
            h = op("pe", lambda e, kc=kc: e.transpose(
                out=psb(tb[kc // 8])[:, kc % 8, 0:m], in_=h2tmp[hbi][0:m, kc * 128:(kc + 1) * 128],
                identity=ident[0:m, 0:m]), deps if kc == 0 else ())
        b_h2tmp[hbi].read(h); bank[tb[0]].wrote(h); bank[tb[1]].wrote(h)
        h = op("act", lambda e: e.activation(
            out=h2T[:, 0:8, cols], in_=psb(tb[0])[:, :, 0:m], func=AF.Copy), R([bank[tb[0]]]) + W([b_h2T[ti]]))
        bank[tb[0]].read(h); b_h2T[ti].wrote(h, keep=True)
        h = op("dve", lambda e: e.tensor_copy(
            out=h2T[:, 8:16, cols], in_=psb(tb[1])[:, :, 0:m]), R([bank[tb[1]]]) + b_h2T[ti].wdeps(False))
        bank[tb[1]].read(h); b_h2T[ti].wrote(h, keep=True)

    def wo_mm(n, ti, kcs, bk):
        cols, m = tile_cols(ti)
        deps = [wo_load[n]] + R([b_mix[c] for c in kcs]) + (W([bank[bk]]) if kcs[0] == 0 else [])
        for kc in kcs:
            h = op("pe", lambda e, kc=kc: e.matmul(
                ps[0:m, bk * 512:(bk + 1) * 512], lhsT=mixT[:, kc, cols], rhs=wob[n % 2][:, kc, :],
                start=(kc == 0), stop=(kc == 15)), deps if kc == kcs[0] else ())
        bank[bk].wrote(h)
        b_wob[n % 2].read(h)
        for c in kcs:
            b_mix[c].read(h)

    def wo_evac(n, ti, bk):
        cols, m = tile_cols(ti)
        if ti < 8:
            xr = xres[:, ti, n * 512:(n + 1) * 512]; bx = b_xres[ti]
        else:
            xr = xhalo[0:2, n * 512:(n + 1) * 512]; bx = b_xhalo
        h = op("dve", lambda e: e.tensor_tensor(out=xr, in0=ps[0:m, bk * 512:(bk + 1) * 512], in1=xr, op=ALU.add),
               R([bank[bk], bx]))
        bank[bk].read(h); bx.wrote(h)

    for ti in range(4):
        wo_mm(0, ti, list(range(8)), ti)
    hst = stats_part(1, 4)
    wo_mm(0, 7, list(range(8)), 7)
    normalize_part(1, hst)
    for ti in range(4, 7):
        wo_mm(0, ti, list(range(8)), ti)
    for c in range(8, 16):
        for ti in range(8):
            wo_mm(0, ti, [c], ti)
    for ti in range(8):
        wo_evac(0, ti, ti)
    unit = 0
    for n in range(4):
        for ti in range(9):
            if n == 0 and ti < 8:
                continue
            bk = unit % 4
            unit += 1
            wo_mm(n, ti, list(range(16)), bk)
            wo_evac(n, ti, bk)
            if n == 3:
                n2_stats(ti)
                if ti >= 1:
                    n2_scale(ti - 1)
                if ti >= 2:
                    n2_transpose(ti - 2)
        if n + 2 < 4:
            issue_wo(n + 2)
        if n == 2:
            issue_gu(3)
    u_pre = {}
    su0 = gslot(0, 1)
    deps = [gu_load[(0, 1)]] + R(b_h2T[0:4]) + W([bank[3]])
    for kc in range(16):
        h = op("pe", lambda e, kc=kc: e.matmul(
            ps[:, 1536:1536 + 512], lhsT=gub[:, su0, kc, :], rhs=h2T[:, kc, 1:513], start=(kc == 0), stop=(kc == 15)),
            deps if kc == 0 else ())
    bank[3].wrote(h); b_gub[su0].read(h)
    for bb in b_h2T[0:4]:
        bb.read(h)
    u_pre[0] = True

    n2_scale(8)
    n2_transpose(7)
    n2_transpose(8)

    issue_gu(6)
    w_down_v = dt["w_down"].rearrange("(g kc p) n -> g p kc n", g=NGRP, kc=GRP, p=128)
    wd_load = {}
    wd_issued = [0]

    def issue_wd(upto):
        while wd_issued[0] < min(upto, 4 * NGRP):
            k = wd_issued[0]
            gi, n = k // 4, k % 4
            hh = dma("pool", f"wd{k % 2}", wdb[k % 2], w_down_v[gi, :, :, n * 512:(n + 1) * 512], W([b_wdb[k % 2]]))
            b_wdb[k % 2].wrote(hh)
            wd_load[k] = hh
            wd_issued[0] += 1

    hwf = [None]

    FPARTS = [(0, 384), (384, 384), (768, 256)]

    def ffn_chunk(c):
        gi, cl = c // GRP, c % GRP
        ab = gi % 2
        fb = c % 2
        sg = gslot(c, 0)
        su = gslot(c, 1)
        gb = [bank[0], bank[1], bank[2]]
        ub = [bank[3], bank[4]]
        deps = [gu_load[(c, 0)]] + R(b_h2T) + W(gb)
        first = True
        for kc in range(16):
            for r, (o0, cnt) in enumerate(FPARTS):
                h = op("pe", lambda e, kc=kc, r=r, o0=o0, cnt=cnt: e.matmul(
                    ps[:, r * 512:r * 512 + cnt + 2], lhsT=gub[:, sg, kc, :], rhs=h2T[:, kc, o0:o0 + cnt + 2],
                    start=(kc == 0), stop=(kc == 15)), deps if first else ())
                first = False
        for bb in gb:
            bb.wrote(h)
        b_gub[sg].read(h)
        u_rg = _ranges(1, 1024)
        if c == 0 and u_pre.get(0):
            u_rg = u_rg[1:]
            deps = [gu_load[(c, 1)]] + W([bank[4]])
        else:
            deps = [gu_load[(c, 1)]] + W(ub)
        first = True
        for kc in range(16):
            for (cc, m, off) in u_rg:
                h = op("pe", lambda e, kc=kc, cc=cc, m=m, off=off: e.matmul(
                    ps[:, 1536 + off:1536 + off + m], lhsT=gub[:, su, kc, :], rhs=h2T[:, kc, cc:cc + m],
                    start=(kc == 0), stop=(kc == 15)), deps if first else ())
                first = False
        for bb in (ub[1:] if (c == 0 and u_pre.get(0)) else ub):
            bb.wrote(h)
        b_gub[su].read(h)
        for bb in b_h2T:
            bb.read(h)
        issue_gu(2 * (c + 4))
        a = ftmp[fb]
        hprev = None
        for r, (o0, cnt) in enumerate(FPARTS):
            ar = a[:, o0:o0 + cnt]
            g0 = r * 512
            h = op("act", lambda e: e.activation(out=ar, in_=ps[:, g0 + 1:g0 + 1 + cnt], func=AF.Identity,
                                                 scale=par[:, P_FW1 + c:P_FW1 + c + 1], bias=par[:, P_FB + c:P_FB + c + 1]),
                   R([gb[r], b_small]) + (W([b_ftmp[fb]]) if r == 0 else []))
            gb[r].read(h); b_ftmp[fb].wrote(h, keep=True)
            h = op("dve", lambda e: e.scalar_tensor_tensor(out=ar, in0=ps[:, g0:g0 + cnt], scalar=par[:, P_FW0 + c:P_FW0 + c + 1], in1=ar,
                                                           op0=ALU.mult, op1=ALU.add), [h] + R([gb[r]]))
            h = op("dve", lambda e: e.scalar_tensor_tensor(out=ar, in0=ps[:, g0 + 2:g0 + 2 + cnt], scalar=par[:, P_FW2 + c:P_FW2 + c + 1], in1=ar,
                                                           op0=ALU.mult, op1=ALU.add), [h])
            gb[r].read(h); b_ftmp[fb].wrote(h, keep=True)
            hprev = h
        h = op("act", lambda e: e.activation(out=a, in_=a, func=AF.Silu), R([b_ftmp[fb]]))
        b_ftmp[fb].wrote(h)
        h2 = op("dve", lambda e: e.tensor_tensor(out=actb[ab][:, cl, :], in0=ps[:, 1536:1536 + 1024], in1=a, op=ALU.mult),
                [h] + R(ub) + W([b_act[ab][cl]]))
        for bb in ub:
            bb.read(h2)
        b_ftmp[fb].read(h2); b_act[ab][cl].wrote(h2)

    out_h = []

    fin_hr = {}

    ST_PART = 100

    def final_stats(ti, panel=None):
        if hwf[0] is None:
            hwf[0] = dma("sp", "wfbc", wfbc, dt["final_norm_w"].partition_broadcast(128), W([b_wfbc]))
            b_wfbc.wrote(hwf[0])
        if panel is None:
            src = xres[:, ti, :]; dst = junkG; acc = st[:, ST_SSQ3 + ti:ST_SSQ3 + ti + 1]
        else:
            src = xres[:, ti, panel * 512:(panel + 1) * 512]; dst = junkG[:, panel * 512:(panel + 1) * 512]
            acc = st[:, ST_PART + panel:ST_PART + panel + 1]
        hs = op("act", lambda e: e.activation(out=dst, in_=src, func=AF.Square, accum_out=acc),
                R([b_xres[ti], b_small]) + (W([b_junkG]) if (ti == 0 and panel is None) else b_junkG.wdeps(False)))
        b_junkG.wrote(hs, keep=True); b_xres[ti].read(hs)
        if panel is not None:
            hsq = hs
            if panel == 3:
                hs = op("dve", lambda e: e.reduce_sum(out=st[:, ST_SSQ3 + ti:ST_SSQ3 + ti + 1], in_=st[:, ST_PART:ST_PART + 4],
                                                      axis=mybir.AxisListType.X), [hs])
            hw = op("dve", lambda e: e.tensor_tensor(out=src, in0=src, in1=wfbc[:, panel * 512:(panel + 1) * 512], op=ALU.mult),
                    [hsq] + R([b_xres[ti], b_wfbc]))
            b_xres[ti].wrote(hw); b_wfbc.read(hw)
            if panel < 3:
                return
        hl = op("act", lambda e: e.activation(out=st[:, ST_LN3 + ti:ST_LN3 + ti + 1], in_=st[:, ST_SSQ3 + ti:ST_SSQ3 + ti + 1], func=AF.Ln,
                                              scale=1.0 / D, bias=epsT), [hs])
        fin_hr[ti] = op("act", lambda e: e.activation(out=st[:, ST_R3 + ti:ST_R3 + ti + 1], in_=st[:, ST_LN3 + ti:ST_LN3 + ti + 1], func=AF.Exp, scale=-0.5), [hl])

    def final_out(ti):
        xfull = xres[:, ti, :]
        pieces = [(0, 2048)] if ti < 7 else [(q * 512, 512) for q in range(4)]
        for (c0, cn) in pieces:
            xs = xres[:, ti, c0:c0 + cn]
            if ti < 7:
                hh = op("dve", lambda e: e.scalar_tensor_tensor(out=xs, in0=xs, scalar=st[:, ST_R3 + ti:ST_R3 + ti + 1], in1=wfbc[:, c0:c0 + cn],
                                                                op0=ALU.mult, op1=ALU.mult), [fin_hr[ti]] + R([b_xres[ti], b_wfbc]))
                b_wfbc.read(hh)
            else:
                hh = op("dve", lambda e: e.tensor_scalar_mul(out=xs, in0=xs, scalar1=st[:, ST_R3 + ti:ST_R3 + ti + 1]),
                        [fin_hr[ti]] + R([b_xres[ti]]))
            ho = dma("sp", f"out{ti}_{c0}", dt["y"][ti * 128:(ti + 1) * 128, c0:c0 + cn], xs, [hh])
            out_h.append(ho)
        b_xres[ti].wrote(hh)

    def ffn_down(gi):
        ab = gi % 2
        for n in range(4):
            k = gi * 4 + n
            issue_wd(min(k + 2, 13))
            for ti in range(8):
                bk = 6 + (ti % 2)
                deps = [wd_load[k]] + R(b_act[ab]) + W([bank[bk]])
                for kc in range(GRP):
                    h = op("pe", lambda e, kc=kc: e.matmul(
                        ps[:, bk * 512:(bk + 1) * 512], lhsT=actb[ab][:, kc, ti * 128:(ti + 1) * 128], rhs=wdb[k % 2][:, kc, :],
                        start=(kc == 0), stop=(kc == GRP - 1)), deps if kc == 0 else ())
                bank[bk].wrote(h)
                b_wdb[k % 2].read(h)
                for bb in b_act[ab]:
                    bb.read(h)
                xr = xres[:, ti, n * 512:(n + 1) * 512]
                h = op("dve", lambda e: e.tensor_tensor(out=xr, in0=ps[:, bk * 512:(bk + 1) * 512], in1=xr, op=ALU.add),
                       R([bank[bk], b_xres[ti]]))
                bank[bk].read(h); b_xres[ti].wrote(h)

    def issue_wdx():
        for i in range(2):
            k = 14 + i
            hh = dma("pool", f"wdx{i}", wdx[i], w_down_v[3, :, :, (2 + i) * 512:(3 + i) * 512], W([b_wdx[i]]))
            b_wdx[i].wrote(hh)
            wd_load[k] = hh

    def ffn_down_last():
        gi = NGRP - 1
        ab = gi % 2
        issue_wd(14)
        pan = [wdb[0], wdb[1], wdx[0], wdx[1]]
        bpan = [b_wdb[0], b_wdb[1], b_wdx[0], b_wdx[1]]
        u = 0
        for ti in range(8):
            for n in range(4):
                k = gi * 4 + n
                bk = 6 + (u % 2)
                u += 1
                deps = [wd_load[k]] + R(b_act[ab]) + W([bank[bk]])
                for kc in range(GRP):
                    h = op("pe", lambda e, kc=kc: e.matmul(
                        ps[:, bk * 512:(bk + 1) * 512], lhsT=actb[ab][:, kc, ti * 128:(ti + 1) * 128], rhs=pan[n][:, kc, :],
                        start=(kc == 0), stop=(kc == GRP - 1)), deps if kc == 0 else ())
                bank[bk].wrote(h)
                bpan[n].read(h)
                for bb in b_act[ab]:
                    bb.read(h)
                xr = xres[:, ti, n * 512:(n + 1) * 512]
                h = op("dve", lambda e: e.tensor_tensor(out=xr, in0=ps[:, bk * 512:(bk + 1) * 512], in1=xr, op=ALU.add),
                       R([bank[bk], b_xres[ti]]))
                bank[bk].read(h); b_xres[ti].wrote(h)
                if n == 1 and ti >= 1:
                    final_out(ti - 1)
                if ti == 7:
                    final_stats(ti, n)
            if ti < 7:
                final_stats(ti)
        final_out(7)

    issue_wd(1)
    for c in range(GRP):
        ffn_chunk(c)
    for gi in range(1, NGRP):
        ffn_chunk(gi * GRP)
        ffn_down(gi - 1)
        for cl in range(1, GRP):
            ffn_chunk(gi * GRP + cl)
            if gi == NGRP - 1 and cl == 4:
                issue_wdx()
    ffn_down_last()
    S_.fence("sp", out_h)


_CACHE = {}


def _host_tables():
    slopes = 2.0 ** (-(np.arange(1, 9)))
    ki = np.arange(128)[:, None]
    qq = np.arange(384)[None, :]
    rel = (128 + ki) - qq
    T = np.empty((128, 8, 384), np.float32)
    for h in range(8):
        T[:, h, :] = np.where(np.abs(rel) <= 128, -slopes[h] * np.abs(rel), NEG)
    return T.reshape(128, 8 * 384)


def kernel(x, attn_norm_w, w_in, sink_logits, mix_conv_w, mix_conv_b, attn_out_norm_w, conv_out_norm_w,
           w_out, ffn_norm_w, w_gate, w_up, ffn_conv_w, ffn_conv_b, w_down, final_norm_w):
    x = np.asarray(x, np.float32)
    f = lambda a: np.ascontiguousarray(np.asarray(a, np.float32))
    if "nc" not in _CACHE:
        _CACHE["nc"] = build_program()
    nc = _CACHE["nc"]

    par = np.zeros((128, NPAR), np.float32)
    par[:, 0:8] = f(attn_out_norm_w)[0].reshape(8, 128).T
    par[:, 8:16] = f(conv_out_norm_w)[0].reshape(8, 128).T
    mw = f(mix_conv_w)[0]
    for k in range(3):
        par[:, 16 + 8 * k:24 + 8 * k] = mw[k].reshape(8, 128).T
    par[:, 40:48] = f(mix_conv_b)[0].reshape(8, 128).T
    par[:, 48:56] = np.broadcast_to(f(sink_logits)[0][None, :], (128, 8))
    fw = f(ffn_conv_w)[0]
    for k in range(3):
        par[:, 56 + 44 * k:56 + 44 * (k + 1)] = fw[k].reshape(44, 128).T
    par[:, 56 + 132:56 + 176] = f(ffn_conv_b)[0].reshape(44, 128).T

    biast = _host_tables()
    ident = np.eye(128, dtype=np.float32)
    shared = {
        "w_in": f(w_in)[0], "w_out": f(w_out)[0], "w_gate": f(w_gate)[0], "w_up": f(w_up)[0], "w_down": f(w_down)[0],
        "attn_norm_w": f(attn_norm_w)[0], "ffn_norm_w": f(ffn_norm_w)[0], "final_norm_w": f(final_norm_w),
        "params": par, "biast": biast, "ident": ident,
    }
    in_maps = []
    for core in range(8):
        b, sc = core // 4, core % 4
        t0 = sc * TOK
        xe = np.zeros((EXT, D), np.float32)
        lo, hi = t0 - HALO, t0 + TOK + HALO
        slo, shi = max(lo, 0), min(hi, S)
        xe[slo - lo:shi - lo] = x[b, slo:shi]
        tok = lo + np.arange(EXT)
        valid = (tok >= 0) & (tok < S)
        kbias = np.zeros((128, 11), np.float32)
        for j in range(11):
            for p in range(128):
                c = 128 * j + p
                if c < EXT and not valid[c]:
                    kbias[p, j] = NEG
        hflag = np.array([[1.0 if valid[128] else 0.0], [1.0 if valid[1153] else 0.0]], np.float32)
        m = dict(shared)
        m.update({"x_ext": xe, "kbias": kbias, "hflag": hflag})
        in_maps.append(m)
    res = run_bass_kernel_spmd(nc, in_maps, core_ids=list(range(8)))
    out = np.empty((NB, S, D), np.float32)
    for core in range(8):
        b, sc = core // 4, core % 4
        out[b, sc * TOK:(sc + 1) * TOK] = res.results[core]["y"]
    return out
```

```python
import numpy as np
import concourse.bass as bass
import concourse.mybir as mybir
from concourse.bass_utils import run_bass_kernel_spmd

F32 = mybir.dt.float32
BF16 = mybir.dt.bfloat16
AF = mybir.ActivationFunctionType
ALU = mybir.AluOpType

D = 2048
S = 4096
NB = 2
TOK = 1024
HALO = 129
EXT = TOK + 2 * HALO
NQ = TOK + 2
DFF = 5632
NCH = DFF // 128
GRP = 11
NGRP = NCH // GRP
INW = 4608
EPS = 1e-6
NEG = -30000.0
NPAR = 232

SMALL = 0
A0 = 2560
XRES = A0
H2T = XRES + 65536
MIXT = H2T + 32832
REST = MIXT + 32832
ARENA = REST + 73728
H1T = A0
QT = H1T + 41024
KT = QT + 16416
VTOK = KT + 5128
SQ = VTOK + 5632
RSTDBC = SQ + 16416
assert RSTDBC + 8208 <= MIXT
WBUF = REST
WOB0 = REST + 16384
R2 = REST + 32768
PT = R2
CTMP = PT + 16896
BIAST = CTMP + 12360
assert BIAST + 6144 <= ARENA
XT = R2
XT4 = MIXT
HB = XT + 16384
JUNKA = HB + 12288
assert JUNKA + 4096 <= ARENA
W1BC = WOB0 + 8192
XHALO = R2
H2TMP = XHALO + 8192
JUNKE = H2TMP + 8192
W2BC = JUNKE + 4096
assert W2BC + 8192 <= ARENA
ACT0 = MIXT
WDB = ACT0 + 45056
FTMP = WDB + 22528
GUB = ARENA - 24576
assert FTMP + 8192 <= GUB
WFBC = GUB
JUNKG = GUB + 8192
SM_IDENT = 0
SM_ONES = 256
SM_PAR = 512
SM_KB = 1440
SM_ST = 1504
SM_HF = 2016
SM_EPS = 2024
SM_IDF = 2048


class H:
    __slots__ = ("id", "eng", "count", "dsem", "dval")

    def __init__(self, id, eng):
        self.id = id
        self.eng = eng
        self.count = None
        self.dsem = None
        self.dval = None


class Buf:
    ALL = []

    def __init__(self, name, off=None, size=0, group=None):
        self.name = name
        self.w = {}
        self.r = {}
        self.off = off
        self.size = size
        self.group = group
        self._al = None
        Buf.ALL.append(self)

    def aliases(self):
        if self._al is None:
            self._al = []
            if self.off is not None:
                for b in Buf.ALL:
                    if b is self or b.off is None:
                        continue
                    if self.group is not None and b.group == self.group:
                        continue
                    if b.off < self.off + self.size and self.off < b.off + b.size:
                        self._al.append(b)
        return self._al

    def wdeps(self, include_self=True):
        d = []
        if include_self:
            d += list(self.w.values()) + list(self.r.values())
        for a in self.aliases():
            d += list(a.w.values()) + list(a.r.values())
        return d

    def rdeps(self):
        return list(self.w.values())

    def wrote(self, h, keep=False):
        if not keep:
            self.w = {}
        self.r = {}
        self.w[h.eng if h.dsem is None else ("dma", id(h.dsem))] = h

    def read(self, h):
        self.r[h.eng] = h


class Sched:
    def __init__(self, nc, dry, signaled):
        self.nc = nc
        self.dry = dry
        self.signaled = signaled
        self.n = 0
        self.engs = {"pe": nc.tensor, "act": nc.scalar, "dve": nc.vector, "pool": nc.gpsimd, "sp": nc.sync}
        self.count = {e: 0 for e in self.engs}
        self.waited = {e: {} for e in self.engs}
        self.dslots = {}
        if not dry:
            self.sem = {e: nc.alloc_semaphore("s_" + e) for e in self.engs}

    def _wait(self, eng, deps):
        E = self.engs[eng]
        for d in deps:
            if d is None:
                continue
            if d.dsem is not None:
                key = ("dma", id(d.dsem))
                if self.waited[eng].get(key, 0) < d.dval:
                    E.wait_ge(d.dsem, d.dval)
                    self.waited[eng][key] = d.dval
            else:
                if d.eng == "pe" and eng == "pe":
                    continue
                if self.waited[eng].get(d.eng, 0) < d.count:
                    E.wait_ge(self.sem[d.eng], d.count)
                    self.waited[eng][d.eng] = d.count

    def op(self, eng, fn, deps=()):
        h = H(self.n, eng)
        self.n += 1
        if self.dry:
            for d in deps:
                if d is not None:
                    self.signaled.add(d.id)
            return h
        self._wait(eng, deps)
        inst = fn(self.engs[eng])
        if h.id in self.signaled:
            self.count[eng] += 1
            inst.then_inc(self.sem[eng], 1)
        h.count = self.count[eng]
        return h

    def fence(self, eng, deps):
        if self.dry:
            for d in deps:
                if d is not None and d.dsem is None:
                    self.signaled.add(d.id)
            return
        self._wait(eng, deps)

    def dma(self, queue, slot, out, in_, deps=()):
        h = H(self.n, queue)
        self.n += 1
        if slot not in self.dslots:
            self.dslots[slot] = [None if self.dry else self.nc.alloc_semaphore("d_" + slot), 0]
        st = self.dslots[slot]
        st[1] += 16
        h.dsem = st[0] if not self.dry else slot
        h.dval = st[1]
        if self.dry:
            for d in deps:
                if d is not None:
                    self.signaled.add(d.id)
            return h
        self._wait(queue, deps)
        self.engs[queue].dma_start(out=out, in_=in_).then_inc(st[0], 16)
        return h


def _ranges(c0, n):
    out = []
    off = 0
    while n > 0:
        m = min(512, n)
        out.append((c0, m, off))
        c0 += m
        off += m
        n -= m
    return out


def build_program():
    nc = bass.Bass("TRN2", target_bir_lowering=False)
    dt = {}
    dt["x_ext"] = nc.dram_tensor("x_ext", [EXT, D], F32, kind="ExternalInput").ap()
    dt["w_in"] = nc.dram_tensor("w_in", [D, INW], F32, kind="ExternalInput").ap()
    dt["w_out"] = nc.dram_tensor("w_out", [D, D], F32, kind="ExternalInput").ap()
    dt["w_gate"] = nc.dram_tensor("w_gate", [D, DFF], F32, kind="ExternalInput").ap()
    dt["w_up"] = nc.dram_tensor("w_up", [D, DFF], F32, kind="ExternalInput").ap()
    dt["w_down"] = nc.dram_tensor("w_down", [DFF, D], F32, kind="ExternalInput").ap()
    dt["attn_norm_w"] = nc.dram_tensor("attn_norm_w", [D], F32, kind="ExternalInput").ap()
    dt["ffn_norm_w"] = nc.dram_tensor("ffn_norm_w", [D], F32, kind="ExternalInput").ap()
    dt["final_norm_w"] = nc.dram_tensor("final_norm_w", [D], F32, kind="ExternalInput").ap()
    dt["params"] = nc.dram_tensor("params", [128, NPAR], F32, kind="ExternalInput").ap()
    dt["kbias"] = nc.dram_tensor("kbias", [128, 11], F32, kind="ExternalInput").ap()
    dt["hflag"] = nc.dram_tensor("hflag", [2, 1], F32, kind="ExternalInput").ap()
    dt["biast"] = nc.dram_tensor("biast", [128, 8 * 384], F32, kind="ExternalInput").ap()
    dt["ident"] = nc.dram_tensor("ident", [128, 128], F32, kind="ExternalInput").ap()
    dt["y"] = nc.dram_tensor("y", [TOK, D], F32, kind="ExternalOutput").ap()

    arena = nc.alloc_sbuf_tensor("arena", [128, ARENA // 4], F32)
    ps = nc.alloc_psum_tensor("ps", [128, 4096], F32)

    signaled = set()
    emit(nc, dt, arena, ps, Sched(nc, True, signaled))
    emit(nc, dt, arena, ps, Sched(nc, False, signaled))
    return nc


def emit(nc, dt, arena, ps, S_):
    op, dma = S_.op, S_.dma
    Buf.ALL = []

    def cv(off, shape, dtype):
        n = int(np.prod(shape))
        sz = 2 if dtype == BF16 else 4
        assert off % 4 == 0
        w = (n * sz + 3) // 4
        assert off + 4 * w <= ARENA, (off, shape)
        ap = arena[:, off // 4: off // 4 + w]
        if dtype == BF16:
            ap = ap.bitcast(BF16)[:, 0:n]
        if len(shape) == 2:
            ap = ap.rearrange("p (a b) -> p a b", a=shape[0], b=shape[1])
        elif len(shape) == 3:
            ap = ap.rearrange("p (a b c) -> p a b c", a=shape[0], b=shape[1], c=shape[2])
        return ap

    def psb(bank):
        return ps[:, bank * 512:(bank + 1) * 512].bitcast(BF16).rearrange("p (a b) -> p a b", a=8, b=128)

    ident = cv(SM_IDENT, [128], BF16)
    ones = cv(SM_ONES, [128], BF16)
    par = cv(SM_PAR, [NPAR], F32)
    kb = cv(SM_KB, [11], F32)
    st = cv(SM_ST, [128], F32)
    hf = cv(SM_HF, [1], F32)
    epsT = cv(SM_EPS, [1], F32)
    P_AW, P_CW, P_MW0, P_MW1, P_MW2, P_MB, P_SINK, P_FW0 = 0, 8, 16, 24, 32, 40, 48, 56
    P_FW1, P_FW2, P_FB = P_FW0 + 44, P_FW0 + 88, P_FW0 + 132
    ST_SSQ1, ST_LN1, ST_R1 = 0, 11, 22
    ST_ESINK = 33
    ST_SSQ2, ST_LN2, ST_R2 = 41, 50, 59
    ST_SSQ3, ST_LN3, ST_R3 = 68, 76, 84

    h1T = cv(H1T, [16, EXT], BF16)
    qT = cv(QT, [8, NQ], BF16)
    kT = cv(KT, [2, EXT], BF16)
    vtok = cv(VTOK, [11, 256], BF16)
    sq = cv(SQ, [8, NQ], BF16)
    rstdbc = cv(RSTDBC, [2, NQ], F32)
    mixT = cv(MIXT, [16, NQ], BF16)
    wbuf = cv(WBUF, [6, 16, 128], BF16)
    vpan = cv(WBUF, [16, 256], BF16)
    kpan = cv(WBUF + 8192, [16, 256], BF16)
    wob = [cv(WOB0, [16, 512], BF16), cv(WBUF, [16, 512], BF16)]
    PTb = [cv(PT, [11, 384], BF16), cv(PT + 8448, [11, 384], BF16)]
    u_sb = cv(CTMP, [1032], F32)
    cu = cv(CTMP + 4128, [1032], F32)
    ycv = cv(CTMP + 8256, [1026], F32)
    biasT = cv(BIAST, [8, 384], BF16)
    xt = [cv(XT, [2048], F32), cv(XT + 8192, [2048], F32)]
    NXA = 8
    xa_off = [XT4 + i * 8192 for i in range(4)] + [QT, QT + 8192, SQ, SQ + 8192]
    xa = [cv(o, [2048], F32) for o in xa_off]
    hb = [cv(HB + i * 4096, [2048], BF16) for i in range(3)]
    junkA = cv(JUNKA, [2048], BF16)
    w1bc = cv(W1BC, [2048], F32)
    xres = cv(XRES, [8, 2048], F32)
    h2T = cv(H2T, [16, NQ], BF16)
    xhalo = cv(XHALO, [2048], F32)
    h2tmp = [cv(H2TMP, [2048], BF16), cv(H2TMP + 4096, [2048], BF16)]
    junkE = cv(JUNKE, [2048], BF16)
    w2bc = cv(W2BC, [2048], F32)
    actb = [cv(ACT0, [GRP, 1024], BF16), cv(ACT0 + 22528, [GRP, 1024], BF16)]
    gub = cv(GUB, [6, 16, 128], BF16)
    wdb = [cv(WDB, [GRP, 512], BF16), cv(WDB + 11264, [GRP, 512], BF16)]
    ftmp = [cv(FTMP, [1024], F32), cv(FTMP + 4096, [1024], F32)]
    wfbc = cv(WFBC, [2048], F32)
    junkG = cv(JUNKG, [2048], BF16)
    wdx = [cv(ACT0, [GRP, 512], BF16), cv(ACT0 + 11264, [GRP, 512], BF16)]

    bank = [Buf(f"bank{i}") for i in range(8)]
    b_small = Buf("small")
    b_h1T = [Buf(f"h1T{t}", H1T, 41024, "h1T") for t in range(11)]
    b_qT = [Buf(f"qT{h}", QT + h * NQ * 2, NQ * 2) for h in range(8)]
    b_kT = Buf("kT", KT, 5128)
    b_vtok = Buf("vtok", VTOK, 5632)
    b_wbuf = [Buf(f"wbuf{i}", WBUF + i * 4096, 4096) for i in range(6)]
    b_w1bc = Buf("w1bc", W1BC, 8192)
    b_biasT = Buf("biasT", BIAST, 6144)
    b_xt = [Buf(f"xt{i}", XT + i * 8192, 8192) for i in range(2)]
    b_xa = [Buf(f"xa{i}", xa_off[i], 8192) for i in range(NXA)]
    b_hb = [Buf(f"hb{i}", HB + i * 4096, 4096) for i in range(3)]
    b_junkA = Buf("junkA", JUNKA, 4096)
    b_PT = [Buf(f"PT{i}", PT + i * 8448, 8448) for i in range(2)]
    b_usb = Buf("u_sb", CTMP, 4128)
    b_cu = Buf("cu", CTMP + 4128, 4128)
    b_y = Buf("ycv", CTMP + 8256, 4104)
    b_usbu = [Buf(f"usb_u{u}", CTMP + 4 * 384 * u, 4 * 384) for u in range(3)]
    b_cuu = [Buf(f"cu_u{u}", CTMP + 4128 + 4 * 384 * u, 4 * 384) for u in range(3)]
    b_sq = Buf("sq", SQ, 16416)
    b_rbc = Buf("rstdbc", RSTDBC, 8208)
    b_mix = [Buf(f"mix{c}", MIXT + c * NQ * 2, NQ * 2) for c in range(16)]
    b_xres = [Buf(f"xres{i}", XRES + i * 8192, 8192) for i in range(8)]
    b_h2T = [Buf(f"h2T{t}", H2T, 32832, "h2T") for t in range(9)]
    b_wob = [Buf("wob0", WOB0, 16384), Buf("wob1", WBUF, 16384)]
    b_xhalo = Buf("xhalo", XHALO, 8192)
    b_h2tmp = [Buf(f"h2tmp{i}", H2TMP + i * 4096, 4096) for i in range(2)]
    b_junkE = Buf("junkE", JUNKE, 4096)
    b_w2bc = Buf("w2bc", W2BC, 8192)
    b_act = [[Buf(f"act{g}_{c}", ACT0 + g * 22528 + c * 2048, 2048) for c in range(GRP)] for g in range(2)]
    b_gub = [Buf(f"gub{i}", GUB + i * 4096, 4096) for i in range(6)]
    b_wdb = [Buf(f"wdb{i}", WDB + i * 11264, 11264) for i in range(2)]
    b_ftmp = [Buf(f"ftmp{i}", FTMP + i * 4096, 4096) for i in range(2)]
    b_wfbc = Buf("wfbc", WFBC, 8192)
    b_wdx = [Buf(f"wdx{i}", ACT0 + i * 11264, 11264) for i in range(2)]
    b_junkG = Buf("junkG", JUNKG, 4096)

    def W(bufs):
        d = []
        for b in bufs:
            d += b.wdeps()
        return d

    def R(bufs):
        d = []
        for b in bufs:
            d += b.rdeps()
        return d

    def h1_tiles(c0, n):
        ts = [t for t in range(10) if 128 * t < c0 + n and c0 < 128 * t + 128]
        if c0 + n > 1280:
            ts.append(10)
        return [b_h1T[t] for t in ts]

    hx0 = dma("sp", "xa0", cv(XT4, [2048], F32), dt["x_ext"][0:128, :])
    h = dma("sp", "par", par, dt["params"][:, :])
    b_small.wrote(h, keep=True)
    h = dma("sp", "kb", kb, dt["kbias"][:, :])
    b_small.wrote(h, keep=True)
    h = dma("sp", "hf", hf[0:2, :], dt["hflag"][:, :])
    b_small.wrote(h, keep=True)
    h_w1 = dma("sp", "w1bc", w1bc, dt["attn_norm_w"].partition_broadcast(128))
    b_w1bc.wrote(h_w1)
    h_id = dma("sp", "idl", xt[1][:, 0:128], dt["ident"][:, :])
    b_xt[1].wrote(h_id)
    h = op("dve", lambda e: e.tensor_copy(out=ident, in_=xt[1][:, 0:128]), R([b_xt[1]]))
    b_small.wrote(h, keep=True); b_xt[1].read(h)
    h = op("dve", lambda e: e.memset(ones, 1.0))
    b_small.wrote(h, keep=True)
    h = op("dve", lambda e: e.memset(st, 0.0))
    b_small.wrote(h, keep=True)
    h = op("dve", lambda e: e.memset(epsT, EPS))
    b_small.wrote(h, keep=True)
    h = op("act", lambda e: e.activation(out=st[:, ST_ESINK:ST_ESINK + 8], in_=par[:, P_SINK:P_SINK + 8], func=AF.Exp),
           R([b_small]))
    b_small.wrote(h, keep=True)

    w_in_v = dt["w_in"].rearrange("(kc p) n -> p kc n", p=128)
    blocks = [("v", 0, 1280), ("v", 1, 1408), ("k", 0, 1024), ("k", 1, 1152)]
    blocks += [("q", hh, 128 * hh) for hh in range(8)]
    for g in range(8):
        blocks += [("u", g, 3584 + 128 * g), ("c", g, 2560 + 128 * g), ("b", g, 1536 + 128 * g)]
    wload = {}
    nissued = [0]

    def wslot(i):
        return i if i < 6 else (i - 6) % 4

    def issue_w(upto, extra=()):
        while nissued[0] < min(upto, len(blocks)):
            i = nissued[0]
            s = wslot(i)
            col = blocks[i][2]
            if i in (0, 2):
                pan = cv(WBUF + s * 4096, [16, 256], BF16)
                hh = dma("pool", f"wb{s}", pan, w_in_v[:, :, col:col + 256], W([b_wbuf[s], b_wbuf[s + 1]]) + list(extra))
                b_wbuf[s].wrote(hh); b_wbuf[s + 1].wrote(hh)
                wload[i] = hh; wload[i + 1] = hh
                nissued[0] += 2
                continue
            hh = dma("pool", f"wb{s}", wbuf[:, s], w_in_v[:, :, col:col + 128], W([b_wbuf[s]]) + list(extra))
            b_wbuf[s].wrote(hh)
            wload[i] = hh
            nissued[0] += 1

    tilesA = [(128 * i, 128) for i in range(10)] + [(1280, 2)]

    hxa = {}

    def a_dma(t, extra=()):
        r0, nr = tilesA[t]
        bi = t % NXA
        if t == 0:
            hx = hx0
        else:
            hx = dma("sp", f"xa{bi}", xa[bi][0:nr, :], dt["x_ext"][r0:r0 + nr, :], W([b_xa[bi]]) + list(extra))
        b_xa[bi].wrote(hx)
        hxa[t] = hx

    def a_s1(t):
        r0, nr = tilesA[t]
        bi = t % NXA
        h = op("act", lambda e: e.activation(
            out=junkA[0:nr, :], in_=xa[bi][0:nr, :], func=AF.Square, accum_out=st[0:nr, ST_SSQ1 + t:ST_SSQ1 + t + 1]),
            R([b_xa[bi], b_small]) + b_junkA.wdeps(False))
        b_xa[bi].read(h); b_junkA.wrote(h, keep=True)
        h = op("act", lambda e: e.activation(
            out=st[0:nr, ST_LN1 + t:ST_LN1 + t + 1], in_=st[0:nr, ST_SSQ1 + t:ST_SSQ1 + t + 1], func=AF.Ln,
            scale=1.0 / D, bias=epsT[0:nr, :]), [h])
        h = op("act", lambda e: e.activation(
            out=st[0:nr, ST_R1 + t:ST_R1 + t + 1], in_=st[0:nr, ST_LN1 + t:ST_LN1 + t + 1], func=AF.Exp, scale=-0.5), [h])
        return h

    def a_s2(t, hr):
        r0, nr = tilesA[t]
        bi = t % 3
        xi = t % NXA
        h = op("dve", lambda e: e.scalar_tensor_tensor(
            out=hb[bi][0:nr, :], in0=xa[xi][0:nr, :], scalar=st[0:nr, ST_R1 + t:ST_R1 + t + 1], in1=w1bc[0:nr, :],
            op0=ALU.mult, op1=ALU.mult), [hr] + R([b_xa[xi], b_w1bc]) + W([b_hb[bi]]))
        b_xa[xi].read(h); b_hb[bi].wrote(h); b_w1bc.read(h)

    def a_s3(t):
        r0, nr = tilesA[t]
        bi = t % 2
        hi = t % 3
        bk = [2 * bi, 2 * bi + 1]
        deps = R([b_hb[hi], b_small]) + W([bank[bk[0]], bank[bk[1]]])
        for kc in range(16):
            h = op("pe", lambda e, kc=kc: e.transpose(
                out=psb(bk[kc // 8])[:, kc % 8, 0:nr], in_=hb[hi][0:nr, kc * 128:(kc + 1) * 128],
                identity=ident[0:nr, 0:nr]), deps if kc == 0 else ())
        b_hb[hi].read(h); bank[bk[0]].wrote(h); bank[bk[1]].wrote(h)

    def a_s4(t):
        r0, nr = tilesA[t]
        bi = t % 2
        bk = [2 * bi, 2 * bi + 1]
        h = op("act", lambda e: e.activation(
            out=h1T[:, 0:8, r0:r0 + nr], in_=psb(bk[0])[:, :, 0:nr], func=AF.Copy), R([bank[bk[0]]]))
        bank[bk[0]].read(h); b_h1T[t].wrote(h, keep=True)
        h = op("dve", lambda e: e.tensor_copy(
            out=h1T[:, 8:16, r0:r0 + nr], in_=psb(bk[1])[:, :, 0:nr]), R([bank[bk[1]]]))
        bank[bk[1]].read(h); b_h1T[t].wrote(h, keep=True)

    kcount = [0]

    def k_range(g, rg):
        c0, m, _ = rg
        bk = 4 + (kcount[0] % 2)
        kcount[0] += 1
        deps = [wload[2 + g]] + R(h1_tiles(c0, m)) + W([bank[bk]])
        for kc in range(16):
            h = op("pe", lambda e, kc=kc: e.matmul(
                ps[:, bk * 512: bk * 512 + m], lhsT=kpan[:, kc, g * 128:(g + 1) * 128], rhs=h1T[:, kc, c0:c0 + m],
                start=(kc == 0), stop=(kc == 15)), deps if kc == 0 else ())
        bank[bk].wrote(h); b_wbuf[2 + g].read(h)
        for bb in h1_tiles(c0, m):
            bb.read(h)
        if kcount[0] % 2 == 0:
            h2 = op("act", lambda e: e.activation(out=kT[:, g, c0:c0 + m], in_=ps[:, bk * 512: bk * 512 + m], func=AF.Copy),
                    R([bank[bk]]) + b_kT.wdeps(False))
        else:
            h2 = op("dve", lambda e: e.tensor_copy(out=kT[:, g, c0:c0 + m], in_=ps[:, bk * 512: bk * 512 + m]),
                    R([bank[bk]]) + b_kT.wdeps(False))
        bank[bk].read(h2); b_kT.wrote(h2, keep=True)

    def v_block(j):
        nk = 128 if j < 10 else 2
        vb = 6 + (j % 2)
        deps = [wload[0], wload[1]] + R([b_h1T[j]]) + W([bank[vb]])
        for kc in range(16):
            h = op("pe", lambda e, kc=kc: e.matmul(
                ps[0:nk, vb * 512: vb * 512 + 256], lhsT=h1T[:, kc, 128 * j:128 * j + nk], rhs=vpan[:, kc, :],
                start=(kc == 0), stop=(kc == 15)), deps if kc == 0 else ())
        bank[vb].wrote(h); b_h1T[j].read(h); b_wbuf[0].read(h); b_wbuf[1].read(h)
        if j % 2 == 0:
            h2 = op("dve", lambda e: e.tensor_copy(out=vtok[0:nk, j, :], in_=ps[0:nk, vb * 512: vb * 512 + 256]),
                    R([bank[vb]]) + b_vtok.wdeps(False))
        else:
            h2 = op("act", lambda e: e.activation(out=vtok[0:nk, j, :], in_=ps[0:nk, vb * 512: vb * 512 + 256], func=AF.Copy),
                    R([bank[vb]]) + b_vtok.wdeps(False))
        bank[vb].read(h2); b_vtok.wrote(h2, keep=True)

    krg = _ranges(0, EXT)
    hrs = {}
    a_dma(0)
    issue_w(2)
    a_dma(1)
    a_dma(2)
    for t in range(3, NXA):
        a_dma(t, extra=[] if t == 3 else ([hxa[t - 2]] if t >= 6 else [wload[1]]))
    issue_w(4, extra=[hxa[3]])
    hrs[0] = a_s1(0)
    hrs[1] = a_s1(1)
    a_s2(0, hrs[0])
    VD = 2
    for t in range(11):
        if t >= 1:
            a_s4(t - 1)
        if t + 2 < 11:
            hrs[t + 2] = a_s1(t + 2)
        if t + 1 < 11:
            a_s2(t + 1, hrs[t + 1])
        if t + NXA < 11:
            a_dma(t + NXA, extra=[hxa[t + NXA - 2]])
        if t == 3:
            issue_w(6, extra=[hxa[10]])
        a_s3(t)
        if t >= VD:
            v_block(t - VD)
        if t == 5:
            k_range(0, krg[0]); k_range(1, krg[0])
        if t == 9:
            k_range(0, krg[1]); k_range(1, krg[1])
    a_s4(10)
    for j in range(11 - VD, 11):
        v_block(j)
    k_range(0, krg[2]); k_range(1, krg[2])
    bi_ = 4
    issue_w(10)

    for half in range(2):
        hh = dma("sp", "bt", xt[0][:, 0:1536], dt["biast"][:, half * 1536:(half + 1) * 1536], W([b_xt[0]]) + [hxa[10]])
        b_xt[0].wrote(hh)
        h = op("dve", lambda e, half=half: e.tensor_copy(
            out=biasT[:, half * 4:(half + 1) * 4, :].rearrange("p a b -> p (a b)"), in_=xt[0][:, 0:1536]),
            R([b_xt[0]]) + b_biasT.wdeps(False))
        b_xt[0].read(h); b_biasT.wrote(h, keep=True)

    def win_block(i, set_, c0, n, b0=None):
        s = wslot(i)
        w3 = (n + 2) // 3
        rg = [(c0 + r * w3, w3, r * 512) for r in range(3)]
        if b0 is None:
            b0 = 3 * set_
        bks = [bank[b0 + k] for k in range(3)]
        deps = [wload[i]] + R(b_h1T) + W(bks)
        first = True
        for kc in range(16):
            for (cc, m, off) in rg:
                h = op("pe", lambda e, kc=kc, cc=cc, m=m, off=off: e.matmul(
                    ps[:, b0 * 512 + off: b0 * 512 + off + m], lhsT=wbuf[:, s, kc, :], rhs=h1T[:, kc, cc:cc + m],
                    start=(kc == 0), stop=(kc == 15)), deps if first else ())
                first = False
        for bb in bks:
            bb.wrote(h)
        b_wbuf[s].read(h)
        for bb in b_h1T:
            bb.read(h)
        issue_w(i + 5)
        return h, bks

    def ps3(b0, w3):
        return ps[:, b0 * 512:(b0 + 3) * 512].rearrange("p (a b) -> p a b", a=3, b=512)[:, :, 0:w3]

    def v3(ap, w3):
        return ap[:, 0:3 * w3].rearrange("p (a b) -> p a b", a=3, b=w3)

    set_ = 0
    for hq in range(8):
        h, bks = win_block(bi_, set_, 128, NQ)
        h = op("act", lambda e, hq=hq, set_=set_: e.activation(
            out=v3(qT[:, hq, :], 342), in_=ps3(3 * set_, 342), func=AF.Copy, scale=float(128 ** -0.5)),
            R(bks) + W([b_qT[hq]]))
        for bb in bks:
            bb.read(h)
        b_qT[hq].wrote(h)
        bi_ += 1; set_ ^= 1

    scount = [0]
    ucount = [0]

    pend = {}

    def att_S(hq, j):
        g = hq // 4
        pb = hq % 2
        PTh = PTb[pb]
        nk = 128 if j < 10 else 2
        qlo = max(128, 128 * (j - 1)); qhi = min(128 + NQ, 128 * (j + 2))
        nq = qhi - qlo
        tc0 = qlo - 128 * (j - 1)
        paired = 2 <= j <= 7
        if paired and j % 2 == 0 and scount[0] % 2 == 1:
            scount[0] += 1
        sb = 4 + (scount[0] % 4)
        scount[0] += 1
        deps = R([b_kT, b_qT[hq], b_biasT, b_small]) + W([bank[sb]])
        op("pe", lambda e: e.matmul(
            ps[0:nk, sb * 512: sb * 512 + nq], lhsT=kT[:, g, 128 * j:128 * j + nk], rhs=qT[:, hq, qlo - 128:qlo - 128 + nq],
            start=True, stop=False), deps)
        h = op("pe", lambda e: e.matmul(
            ps[0:nk, sb * 512: sb * 512 + nq], lhsT=ident[:, 0:nk], rhs=biasT[:, hq, tc0:tc0 + nq],
            start=False, stop=True))
        bank[sb].wrote(h)
        b_kT.read(h); b_qT[hq].read(h); b_biasT.read(h)
        if paired and j % 2 == 0:
            pend[hq] = sb
            return
        if paired:
            s0 = pend.pop(hq)
            assert sb == s0 + 1 and nq == 384 and tc0 == 0
            h = op("act", lambda e: e.activation(
                out=PTh[:, j - 1:j + 1, 0:384],
                in_=ps[:, s0 * 512:(s0 + 2) * 512].rearrange("p (a b) -> p a b", a=2, b=512)[:, :, 0:384],
                func=AF.Exp), R([bank[s0], bank[sb]]))
            bank[s0].read(h); bank[sb].read(h); b_PT[pb].wrote(h, keep=True)
            return
        h = op("act", lambda e: e.activation(
            out=PTh[0:nk, j, tc0:tc0 + nq], in_=ps[0:nk, sb * 512: sb * 512 + nq], func=AF.Exp, bias=kb[0:nk, j:j + 1]),
            (R([bank[sb]]) + W([b_PT[pb]])) if j == 0 else R([bank[sb]]))
        bank[sb].read(h); b_PT[pb].wrote(h, keep=True)

    def att_QB(hq, i, first_of_unit, pvb, dnb, u):
        g = hq // 4
        pb = hq % 2
        PTh = PTb[pb]
        nqb = 128 if i < 9 else 2
        js = [j for j in (i - 1, i, i + 1) if 0 <= j <= 10]
        o = 128 * (i - 1) - 384 * u
        deps = R([b_PT[pb], b_vtok, b_small]) + (W([bank[pvb], bank[dnb]]) if first_of_unit else [])
        first = True
        for which in range(2):
            bk = pvb if which == 0 else dnb
            for jj, j in enumerate(js):
                nk = 128 if j < 10 else 2
                tc = 128 * (i - j + 1)
                lhs = vtok[0:nk, j, g * 128:(g + 1) * 128] if which == 0 else ones[0:nk, :]
                h = op("pe", lambda e: e.matmul(
                    ps[:, bk * 512 + o: bk * 512 + o + nqb], lhsT=lhs, rhs=PTh[0:nk, j, tc:tc + nqb],
                    start=(jj == 0), stop=(jj == len(js) - 1)), deps if first else ())
                first = False
        bank[pvb].wrote(h); bank[dnb].wrote(h)
        b_PT[pb].read(h); b_vtok.read(h)

    def att_C(hq, u, pvb, dnb):
        n = 384 if u < 2 else 258
        c0 = 384 * u
        rr = u_sb[:, c0:c0 + n]
        aa = cu[:, c0:c0 + n]
        h = op("act", lambda e: e.activation(out=rr, in_=ps[:, dnb * 512: dnb * 512 + n], func=AF.Ln,
                                             bias=st[:, ST_ESINK + hq:ST_ESINK + hq + 1]),
               R([bank[dnb], b_small]) + W([b_usbu[u]]))
        b_usbu[u].wrote(h); bank[dnb].read(h)
        h = op("act", lambda e: e.activation(out=rr, in_=rr, func=AF.Exp, scale=-1.0), [h])
        b_usbu[u].wrote(h)
        h = op("dve", lambda e: e.tensor_tensor(out=aa, in0=ps[:, pvb * 512: pvb * 512 + n], in1=rr, op=ALU.mult),
               [h] + R([bank[pvb]]) + W([b_cuu[u]]))
        bank[pvb].read(h); b_usbu[u].read(h); b_cuu[u].wrote(h)
        h1 = op("dve", lambda e: e.tensor_scalar_mul(out=mixT[:, hq, c0:c0 + n], in0=aa, scalar1=par[:, P_AW + hq:P_AW + hq + 1]),
                [h] + R([b_small]) + (W([b_mix[hq]]) if u == 0 else b_mix[hq].wdeps(False)))
        b_mix[hq].wrote(h1, keep=True); b_cuu[u].read(h1)
        h2 = op("dve", lambda e: e.tensor_tensor(out=sq[:, hq, c0:c0 + n], in0=aa, in1=aa, op=ALU.mult),
                [h] + b_sq.wdeps(False))
        b_sq.wrote(h2, keep=True); b_cuu[u].read(h2)

    hxr = {}

    def load_xres(i):
        hx = dma("sp", f"xr{i}", xres[:, i, :], dt["x_ext"][129 + 128 * i:257 + 128 * i, :], W([b_xres[i]]))
        b_xres[i].wrote(hx)
        hxr[i] = hx

    for j in range(11):
        att_S(0, j)
    for hq in range(8):
        nxt = list(range(11)) if hq + 1 < 8 else []
        for i in range(1, 10):
            u = min((i - 1) // 3, 2)
            first_of_unit = (i - 1) % 3 == 0 and i < 9
            if first_of_unit:
                pvb, dnb = (0, 1) if ucount[0] % 2 == 0 else (2, 3)
                ucount[0] += 1
            for _ in range(2 if i in (1, 5) else 1):
                if nxt:
                    att_S(hq + 1, nxt.pop(0))
            att_QB(hq, i, first_of_unit, pvb, dnb, u)
            if i in (3, 6, 9):
                att_C(hq, u, pvb, dnb)

    load_xres(6)
    load_xres(7)

    def stats_part(half, b0=0):
        rg = _ranges(0, NQ)
        bks = [bank[b0], bank[b0 + 1], bank[b0 + 2]]
        deps = R([b_sq, b_small]) + W(bks)
        first = True
        for c in range(8):
            for (cc, m, off) in rg:
                h = op("pe", lambda e, c=c, cc=cc, m=m, off=off: e.matmul(
                    ps[:, b0 * 512 + off:b0 * 512 + off + m], lhsT=ones, rhs=sq[:, c, cc:cc + m], start=(c == 0), stop=(c == 7)),
                    deps if first else ())
                first = False
        for bb in bks:
            bb.wrote(h)
        b_sq.read(h)
        h = op("act", lambda e: e.activation(out=rstdbc[:, half, :], in_=ps[:, b0 * 512:b0 * 512 + NQ], func=AF.Ln, scale=1.0 / 1024, bias=epsT),
               R(bks + [b_small]) + W([b_rbc]))
        for bb in bks:
            bb.read(h)
        b_rbc.wrote(h, keep=True)
        h = op("act", lambda e: e.activation(out=rstdbc[:, half, :], in_=rstdbc[:, half, :], func=AF.Exp, scale=-0.5), [h])
        b_rbc.wrote(h, keep=True)
        return h

    def normalize_part(half, h):
        for c in range(8):
            cc = half * 8 + c
            h2 = op("dve", lambda e, cc=cc: e.tensor_tensor(out=mixT[:, cc, :], in0=mixT[:, cc, :], in1=rstdbc[:, half, :], op=ALU.mult),
                    [h] + R([b_mix[cc]]))
            b_mix[cc].wrote(h2); b_rbc.read(h2)


    w_out_v = dt["w_out"].rearrange("(kc p) n -> p kc n", p=128)
    wo_load = {}

    def issue_wo(n, extra=()):
        hh = dma("pool", f"wo{n % 2}", wob[n % 2], w_out_v[:, :, n * 512:(n + 1) * 512], W([b_wob[n % 2]]) + list(extra))
        b_wob[n % 2].wrote(hh)
        wo_load[n] = hh

    for g in range(8):
        ub0 = 4 if g == 0 else 3 * set_
        h, bks = win_block(bi_, set_, 127, 1028, b0=ub0)
        h = op("act", lambda e, ub0=ub0: e.activation(out=v3(u_sb, 343), in_=ps3(ub0, 343), func=AF.Copy),
               R(bks) + W([b_usb]))
        for bb in bks:
            bb.read(h)
        b_usb.wrote(h)
        bi_ += 1; set_ ^= 1
        if g == 0:
            normalize_part(0, stats_part(0, 3 * set_))
        h, bks = win_block(bi_, set_, 127, 1028)
        h = op("dve", lambda e, set_=set_: e.tensor_tensor(out=v3(cu, 343), in0=ps3(3 * set_, 343), in1=v3(u_sb, 343), op=ALU.mult),
               R(bks + [b_usb]) + W([b_cu]))
        for bb in bks:
            bb.read(h)
        b_usb.read(h); b_cu.wrote(h)
        bi_ += 1; set_ ^= 1
        h = op("act", lambda e, g=g: e.activation(out=ycv, in_=cu[:, 1:1027], func=AF.Identity,
                                                  scale=par[:, P_MW1 + g:P_MW1 + g + 1], bias=par[:, P_MB + g:P_MB + g + 1]),
               R([b_cu, b_small]) + W([b_y]))
        b_cu.read(h); b_y.wrote(h)
        h = op("dve", lambda e, g=g: e.scalar_tensor_tensor(out=ycv, in0=cu[:, 0:1026], scalar=par[:, P_MW0 + g:P_MW0 + g + 1], in1=ycv,
                                                            op0=ALU.mult, op1=ALU.add), R([b_y, b_cu]))
        b_y.wrote(h); b_cu.read(h)
        h = op("dve", lambda e, g=g: e.scalar_tensor_tensor(out=ycv, in0=cu[:, 2:1028], scalar=par[:, P_MW2 + g:P_MW2 + g + 1], in1=ycv,
                                                            op0=ALU.mult, op1=ALU.add), [h])
        b_y.wrote(h); b_cu.read(h)
        hy = h
        h, bks = win_block(bi_, set_, 128, NQ)
        h = op("dve", lambda e, set_=set_: e.tensor_tensor(out=v3(u_sb, 342), in0=ps3(3 * set_, 342), in1=v3(ycv, 342), op=ALU.mult),
               [hy] + R(bks) + W([b_usb]))
        for bb in bks:
            bb.read(h)
        b_usb.wrote(h); b_y.read(h)
        bi_ += 1; set_ ^= 1
        h1 = op("act", lambda e, g=g: e.activation(out=mixT[:, 8 + g, :], in_=u_sb[:, 0:NQ], func=AF.Copy, scale=par[:, P_CW + g:P_CW + g + 1]),
                R([b_usb, b_small]) + W([b_mix[8 + g]]))
        b_mix[8 + g].wrote(h1); b_usb.read(h1)
        h2 = op("act", lambda e, g=g: e.activation(out=sq[:, g, :], in_=u_sb[:, 0:NQ], func=AF.Square),
                (R([b_usb]) + W([b_sq])) if g == 0 else (R([b_usb]) + b_sq.wdeps(False)))
        b_sq.wrote(h2, keep=True); b_usb.read(h2)
        if g == 5:
            issue_wo(0)

    for i in range(6):
        load_xres(i)
    issue_wo(1, extra=[hxr[3]])
    hw2 = dma("sp", "w2bc", w2bc, dt["ffn_norm_w"].partition_broadcast(128), W([b_w2bc]))
    b_w2bc.wrote(hw2)
    hx = dma("sp", "xh", xhalo[0:2, :], dt["x_ext"][128:1154:1025, :], W([b_xhalo]))
    b_xhalo.wrote(hx)

    w_gate_v = dt["w_gate"].rearrange("(kc p) n -> p kc n", p=128)
    w_up_v = dt["w_up"].rearrange("(kc p) n -> p kc n", p=128)
    gu_load = {}
    gu_issued = [0]

    def gslot(c, k):
        return (2 * c + k + 3) % 6

    def issue_gu(upto_halves):
        while gu_issued[0] < min(upto_halves, 2 * NCH):
            c, k = gu_issued[0] // 2, gu_issued[0] % 2
            src = w_gate_v if k == 0 else w_up_v
            s = gslot(c, k)
            hh = dma("pool", f"gu{s}", gub[:, s], src[:, :, c * 128:(c + 1) * 128], W([b_gub[s]]))
            b_gub[s].wrote(hh)
            gu_load[(c, k)] = hh
            gu_issued[0] += 1

    def tile_cols(ti):
        if ti < 8:
            return slice(1 + 128 * ti, 129 + 128 * ti), 128
        return slice(0, 1026, 1025), 2

    n2_hr = {}

    def n2_stats(ti):
        cols, m = tile_cols(ti)
        bx = b_xres[ti] if ti < 8 else b_xhalo
        xfull = xres[:, ti, :] if ti < 8 else xhalo[0:2, :]
        hs = op("act", lambda e: e.activation(
            out=junkE[0:m, :], in_=xfull, func=AF.Square, accum_out=st[0:m, ST_SSQ2 + ti:ST_SSQ2 + ti + 1]),
            R([bx, b_small]) + b_junkE.wdeps(False))
        b_junkE.wrote(hs, keep=True); bx.read(hs)
        hl = op("act", lambda e: e.activation(
            out=st[0:m, ST_LN2 + ti:ST_LN2 + ti + 1], in_=st[0:m, ST_SSQ2 + ti:ST_SSQ2 + ti + 1], func=AF.Ln,
            scale=1.0 / D, bias=epsT[0:m, :]), [hs])
        n2_hr[ti] = op("act", lambda e: e.activation(
            out=st[0:m, ST_R2 + ti:ST_R2 + ti + 1], in_=st[0:m, ST_LN2 + ti:ST_LN2 + ti + 1], func=AF.Exp, scale=-0.5), [hl])

    def n2_scale(ti):
        cols, m = tile_cols(ti)
        bx = b_xres[ti] if ti < 8 else b_xhalo
        xfull = xres[:, ti, :] if ti < 8 else xhalo[0:2, :]
        hbi = ti % 2
        hr = n2_hr[ti]
        if ti == 8:
            hr = op("dve", lambda e: e.tensor_tensor(out=st[0:2, ST_R2 + 8:ST_R2 + 9], in0=st[0:2, ST_R2 + 8:ST_R2 + 9], in1=hf[0:2, :], op=ALU.mult),
                    [hr] + R([b_small]))
        hh = op("dve", lambda e: e.scalar_tensor_tensor(
            out=h2tmp[hbi][0:m, :], in0=xfull, scalar=st[0:m, ST_R2 + ti:ST_R2 + ti + 1], in1=w2bc[0:m, :],
            op0=ALU.mult, op1=ALU.mult), [hr] + R([bx, b_w2bc]) + W([b_h2tmp[hbi]]))
        b_h2tmp[hbi].wrote(hh); bx.read(hh); b_w2bc.read(hh)

    def n2_transpose(ti):
        cols, m = tile_cols(ti)
        hbi = ti % 2
        tb = [4 + 2 * hbi, 5 + 2 * hbi]
        deps = R([b_h2tmp[hbi], b_small]) + W([bank[tb[0]], bank[tb[1]]])
        for kc in range(16):
            h = op("pe", lambda e, kc=kc: e.transpose(
                out=psb(tb[kc // 8])[:, kc % 8, 0:m], in_=h2tmp[hbi][0:m, kc * 128:(kc + 1) * 128],
                identity=ident[0:m, 0:m]), deps if kc == 0 else ())
        b_h2tmp[hbi].read(h); bank[tb[0]].wrote(h); bank[tb[1]].wrote(h)
        h = op("act", lambda e: e.activation(
            out=h2T[:, 0:8, cols], in_=psb(tb[0])[:, :, 0:m], func=AF.Copy), R([bank[tb[0]]]) + W([b_h2T[ti]]))
        bank[tb[0]].read(h); b_h2T[ti].wrote(h, keep=True)
        h = op("dve", lambda e: e.tensor_copy(
            out=h2T[:, 8:16, cols], in_=psb(tb[1])[:, :, 0:m]), R([bank[tb[1]]]) + b_h2T[ti].wdeps(False))
        bank[tb[1]].read(h); b_h2T[ti].wrote(h, keep=True)

    def wo_mm(n, ti, kcs, bk):
        cols, m = tile_cols(ti)
        deps = [wo_load[n]] + R([b_mix[c] for c in kcs]) + (W([bank[bk]]) if kcs[0] == 0 else [])
        for kc in kcs:
            h = op("pe", lambda e, kc=kc: e.matmul(
                ps[0:m, bk * 512:(bk + 1) * 512], lhsT=mixT[:, kc, cols], rhs=wob[n % 2][:, kc, :],
                start=(kc == 0), stop=(kc == 15)), deps if kc == kcs[0] else ())
        bank[bk].wrote(h)
        b_wob[n % 2].read(h)
        for c in kcs:
            b_mix[c].read(h)

    def wo_evac(n, ti, bk):
        cols, m = tile_cols(ti)
        if ti < 8:
            xr = xres[:, ti, n * 512:(n + 1) * 512]; bx = b_xres[ti]
        else:
            xr = xhalo[0:2, n * 512:(n + 1) * 512]; bx = b_xhalo
        h = op("dve", lambda e: e.tensor_tensor(out=xr, in0=ps[0:m, bk * 512:(bk + 1) * 512], in1=xr, op=ALU.add),
               R([bank[bk], bx]))
        bank[bk].read(h); bx.wrote(h)

    for ti in range(4):
        wo_mm(0, ti, list(range(8)), ti)
    hst = stats_part(1, 4)
    wo_mm(0, 7, list(range(8)), 7)
    normalize_part(1, hst)
    for ti in range(4, 7):
        wo_mm(0, ti, list(range(8)), ti)
    for c in range(8, 16):
        for ti in range(8):
            wo_mm(0, ti, [c], ti)
    for ti in range(8):
        wo_evac(0, ti, ti)
    unit = 0
    for n in range(4):
        for ti in range(9):
            if n == 0 and ti < 8:
                continue
            bk = unit % 4
            unit += 1
            wo_mm(n, ti, list(range(16)), bk)
            wo_evac(n, ti, bk)
            if n == 3:
                n2_stats(ti)
                if ti >= 1:
                    n2_scale(ti - 1)
                if ti >= 2:
                    n2_transpose(ti - 2)
        if n + 2 < 4:
            issue_wo(n + 2)
        if n == 2:
            issue_gu(3)
    u_pre = {}
    su0 = gslot(0, 1)
    deps = [gu_load[(0, 1)]] + R(b_h2T[0:4]) + W([bank[3]])
    for kc in range(16):
        h = op("pe", lambda e, kc=kc: e.matmul(
            ps[:, 1536:1536 + 512], lhsT=gub[:, su0, kc, :], rhs=h2T[:, kc, 1:513], start=(kc == 0), stop=(kc == 15)),
            deps if kc == 0 else ())
    bank[3].wrote(h); b_gub[su0].read(h)
    for bb in b_h2T[0:4]:
        bb.read(h)
    u_pre[0] = True

    n2_scale(8)
    n2_transpose(7)
    n2_transpose(8)

    issue_gu(6)
    w_down_v = dt["w_down"].rearrange("(g kc p) n -> g p kc n", g=NGRP, kc=GRP, p=128)
    wd_load = {}
    wd_issued = [0]

    def issue_wd(upto):
        while wd_issued[0] < min(upto, 4 * NGRP):
            k = wd_issued[0]
            gi, n = k // 4, k % 4
            hh = dma("pool", f"wd{k % 2}", wdb[k % 2], w_down_v[gi, :, :, n * 512:(n + 1) * 512], W([b_wdb[k % 2]]))
            b_wdb[k % 2].wrote(hh)
            wd_load[k] = hh
            wd_issued[0] += 1

    hwf = [None]

    FPARTS = [(0, 384), (384, 384), (768, 256)]

    def ffn_chunk(c):
        gi, cl = c // GRP, c % GRP
        ab = gi % 2
        fb = c % 2
        sg = gslot(c, 0)
        su = gslot(c, 1)
        gb = [bank[0], bank[1], bank[2]]
        ub = [bank[3], bank[4]]
        deps = [gu_load[(c, 0)]] + R(b_h2T) + W(gb)
        first = True
        for kc in range(16):
            for r, (o0, cnt) in enumerate(FPARTS):
                h = op("pe", lambda e, kc=kc, r=r, o0=o0, cnt=cnt: e.matmul(
                    ps[:, r * 512:r * 512 + cnt + 2], lhsT=gub[:, sg, kc, :], rhs=h2T[:, kc, o0:o0 + cnt + 2],
                    start=(kc == 0), stop=(kc == 15)), deps if first else ())
                first = False
        for bb in gb:
            bb.wrote(h)
        b_gub[sg].read(h)
        u_rg = _ranges(1, 1024)
        if c == 0 and u_pre.get(0):
            u_rg = u_rg[1:]
            deps = [gu_load[(c, 1)]] + W([bank[4]])
        else:
            deps = [gu_load[(c, 1)]] + W(ub)
        first = True
        for kc in range(16):
            for (cc, m, off) in u_rg:
                h = op("pe", lambda e, kc=kc, cc=cc, m=m, off=off: e.matmul(
                    ps[:, 1536 + off:1536 + off + m], lhsT=gub[:, su, kc, :], rhs=h2T[:, kc, cc:cc + m],
                    start=(kc == 0), stop=(kc == 15)), deps if first else ())
                first = False
        for bb in (ub[1:] if (c == 0 and u_pre.get(0)) else ub):
            bb.wrote(h)
        b_gub[su].read(h)
        for bb in b_h2T:
            bb.read(h)
        issue_gu(2 * (c + 4))
        a = ftmp[fb]
        hprev = None
        for r, (o0, cnt) in enumerate(FPARTS):
            ar = a[:, o0:o0 + cnt]
            g0 = r * 512
            h = op("act", lambda e: e.activation(out=ar, in_=ps[:, g0 + 1:g0 + 1 + cnt], func=AF.Identity,
                                                 scale=par[:, P_FW1 + c:P_FW1 + c + 1], bias=par[:, P_FB + c:P_FB + c + 1]),
                   R([gb[r], b_small]) + (W([b_ftmp[fb]]) if r == 0 else []))
            gb[r].read(h); b_ftmp[fb].wrote(h, keep=True)
            h = op("dve", lambda e: e.scalar_tensor_tensor(out=ar, in0=ps[:, g0:g0 + cnt], scalar=par[:, P_FW0 + c:P_FW0 + c + 1], in1=ar,
                                                           op0=ALU.mult, op1=ALU.add), [h] + R([gb[r]]))
            h = op("dve", lambda e: e.scalar_tensor_tensor(out=ar, in0=ps[:, g0 + 2:g0 + 2 + cnt], scalar=par[:, P_FW2 + c:P_FW2 + c + 1], in1=ar,
                                                           op0=ALU.mult, op1=ALU.add), [h])
            gb[r].read(h); b_ftmp[fb].wrote(h, keep=True)
            hprev = h
        h = op("act", lambda e: e.activation(out=a, in_=a, func=AF.Silu), R([b_ftmp[fb]]))
        b_ftmp[fb].wrote(h)
        h2 = op("dve", lambda e: e.tensor_tensor(out=actb[ab][:, cl, :], in0=ps[:, 1536:1536 + 1024], in1=a, op=ALU.mult),
                [h] + R(ub) + W([b_act[ab][cl]]))
        for bb in ub:
            bb.read(h2)
        b_ftmp[fb].read(h2); b_act[ab][cl].wrote(h2)

    out_h = []

    fin_hr = {}

    ST_PART = 100

    def final_stats(ti, panel=None):
        if hwf[0] is None:
            hwf[0] = dma("sp", "wfbc", wfbc, dt["final_norm_w"].partition_broadcast(128), W([b_wfbc]))
            b_wfbc.wrote(hwf[0])
        if panel is None:
            src = xres[:, ti, :]; dst = junkG; acc = st[:, ST_SSQ3 + ti:ST_SSQ3 + ti + 1]
        else:
            src = xres[:, ti, panel * 512:(panel + 1) * 512]; dst = junkG[:, panel * 512:(panel + 1) * 512]
            acc = st[:, ST_PART + panel:ST_PART + panel + 1]
        hs = op("act", lambda e: e.activation(out=dst, in_=src, func=AF.Square, accum_out=acc),
                R([b_xres[ti], b_small]) + (W([b_junkG]) if (ti == 0 and panel is None) else b_junkG.wdeps(False)))
        b_junkG.wrote(hs, keep=True); b_xres[ti].read(hs)
        if panel is not None:
            if panel < 3:
                return
            hs = op("dve", lambda e: e.reduce_sum(out=st[:, ST_SSQ3 + ti:ST_SSQ3 + ti + 1], in_=st[:, ST_PART:ST_PART + 4],
                                                  axis=mybir.AxisListType.X), [hs])
        hl = op("act", lambda e: e.activation(out=st[:, ST_LN3 + ti:ST_LN3 + ti + 1], in_=st[:, ST_SSQ3 + ti:ST_SSQ3 + ti + 1], func=AF.Ln,
                                              scale=1.0 / D, bias=epsT), [hs])
        fin_hr[ti] = op("act", lambda e: e.activation(out=st[:, ST_R3 + ti:ST_R3 + ti + 1], in_=st[:, ST_LN3 + ti:ST_LN3 + ti + 1], func=AF.Exp, scale=-0.5), [hl])

    def final_out(ti):
        xfull = xres[:, ti, :]
        pieces = [(0, 2048)] if ti < 7 else [(q * 512, 512) for q in range(4)]
        for (c0, cn) in pieces:
            xs = xres[:, ti, c0:c0 + cn]
            hh = op("dve", lambda e: e.scalar_tensor_tensor(out=xs, in0=xs, scalar=st[:, ST_R3 + ti:ST_R3 + ti + 1], in1=wfbc[:, c0:c0 + cn],
                                                            op0=ALU.mult, op1=ALU.mult), [fin_hr[ti]] + R([b_xres[ti], b_wfbc]))
            b_wfbc.read(hh)
            ho = dma("sp", f"out{ti}_{c0}", dt["y"][ti * 128:(ti + 1) * 128, c0:c0 + cn], xs, [hh])
            out_h.append(ho)
        b_xres[ti].wrote(hh)

    def ffn_down(gi):
        ab = gi % 2
        for n in range(4):
            k = gi * 4 + n
            issue_wd(min(k + 2, 13))
            for ti in range(8):
                bk = 6 + (ti % 2)
                deps = [wd_load[k]] + R(b_act[ab]) + W([bank[bk]])
                for kc in range(GRP):
                    h = op("pe", lambda e, kc=kc: e.matmul(
                        ps[:, bk * 512:(bk + 1) * 512], lhsT=actb[ab][:, kc, ti * 128:(ti + 1) * 128], rhs=wdb[k % 2][:, kc, :],
                        start=(kc == 0), stop=(kc == GRP - 1)), deps if kc == 0 else ())
                bank[bk].wrote(h)
                b_wdb[k % 2].read(h)
                for bb in b_act[ab]:
                    bb.read(h)
                xr = xres[:, ti, n * 512:(n + 1) * 512]
                h = op("dve", lambda e: e.tensor_tensor(out=xr, in0=ps[:, bk * 512:(bk + 1) * 512], in1=xr, op=ALU.add),
                       R([bank[bk], b_xres[ti]]))
                bank[bk].read(h); b_xres[ti].wrote(h)

    def issue_wdx():
        for i in range(2):
            k = 14 + i
            hh = dma("pool", f"wdx{i}", wdx[i], w_down_v[3, :, :, (2 + i) * 512:(3 + i) * 512], W([b_wdx[i]]))
            b_wdx[i].wrote(hh)
            wd_load[k] = hh

    def ffn_down_last():
        gi = NGRP - 1
        ab = gi % 2
        issue_wd(14)
        pan = [wdb[0], wdb[1], wdx[0], wdx[1]]
        bpan = [b_wdb[0], b_wdb[1], b_wdx[0], b_wdx[1]]
        u = 0
        for ti in range(8):
            for n in range(4):
                k = gi * 4 + n
                bk = 6 + (u % 2)
                u += 1
                deps = [wd_load[k]] + R(b_act[ab]) + W([bank[bk]])
                for kc in range(GRP):
                    h = op("pe", lambda e, kc=kc: e.matmul(
                        ps[:, bk * 512:(bk + 1) * 512], lhsT=actb[ab][:, kc, ti * 128:(ti + 1) * 128], rhs=pan[n][:, kc, :],
                        start=(kc == 0), stop=(kc == GRP - 1)), deps if kc == 0 else ())
                bank[bk].wrote(h)
                bpan[n].read(h)
                for bb in b_act[ab]:
                    bb.read(h)
                xr = xres[:, ti, n * 512:(n + 1) * 512]
                h = op("dve", lambda e: e.tensor_tensor(out=xr, in0=ps[:, bk * 512:(bk + 1) * 512], in1=xr, op=ALU.add),
                       R([bank[bk], b_xres[ti]]))
                bank[bk].read(h); b_xres[ti].wrote(h)
                if n == 1 and ti >= 1:
                    final_out(ti - 1)
                if ti == 7:
                    final_stats(ti, n)
            if ti < 7:
                final_stats(ti)
        final_out(7)

    issue_wd(1)
    for c in range(GRP):
        ffn_chunk(c)
    for gi in range(1, NGRP):
        ffn_chunk(gi * GRP)
        ffn_down(gi - 1)
        for cl in range(1, GRP):
            ffn_chunk(gi * GRP + cl)
            if gi == NGRP - 1 and cl == 4:
                issue_wdx()
    ffn_down_last()
    S_.fence("sp", out_h)


_CACHE = {}


def _host_tables():
    slopes = 2.0 ** (-(np.arange(1, 9)))
    ki = np.arange(128)[:, None]
    qq = np.arange(384)[None, :]
    rel = (128 + ki) - qq
    T = np.empty((128, 8, 384), np.float32)
    for h in range(8):
        T[:, h, :] = np.where(np.abs(rel) <= 128, -slopes[h] * np.abs(rel), NEG)
    return T.reshape(128, 8 * 384)


def kernel(x, attn_norm_w, w_in, sink_logits, mix_conv_w, mix_conv_b, attn_out_norm_w, conv_out_norm_w,
           w_out, ffn_norm_w, w_gate, w_up, ffn_conv_w, ffn_conv_b, w_down, final_norm_w):
    x = np.asarray(x, np.float32)
    f = lambda a: np.ascontiguousarray(np.asarray(a, np.float32))
    if "nc" not in _CACHE:
        _CACHE["nc"] = build_program()
    nc = _CACHE["nc"]

    par = np.zeros((128, NPAR), np.float32)
    par[:, 0:8] = f(attn_out_norm_w)[0].reshape(8, 128).T
    par[:, 8:16] = f(conv_out_norm_w)[0].reshape(8, 128).T
    mw = f(mix_conv_w)[0]
    for k in range(3):
        par[:, 16 + 8 * k:24 + 8 * k] = mw[k].reshape(8, 128).T
    par[:, 40:48] = f(mix_conv_b)[0].reshape(8, 128).T
    par[:, 48:56] = np.broadcast_to(f(sink_logits)[0][None, :], (128, 8))
    fw = f(ffn_conv_w)[0]
    for k in range(3):
        par[:, 56 + 44 * k:56 + 44 * (k + 1)] = fw[k].reshape(44, 128).T
    par[:, 56 + 132:56 + 176] = f(ffn_conv_b)[0].reshape(44, 128).T

    biast = _host_tables()
    ident = np.eye(128, dtype=np.float32)
    shared = {
        "w_in": f(w_in)[0], "w_out": f(w_out)[0], "w_gate": f(w_gate)[0], "w_up": f(w_up)[0], "w_down": f(w_down)[0],
        "attn_norm_w": f(attn_norm_w)[0], "ffn_norm_w": f(ffn_norm_w)[0], "final_norm_w": f(final_norm_w),
        "params": par, "biast": biast, "ident": ident,
    }
    in_maps = []
    for core in range(8):
        b, sc = core // 4, core % 4
        t0 = sc * TOK
        xe = np.zeros((EXT, D), np.float32)
        lo, hi = t0 - HALO, t0 + TOK + HALO
        slo, shi = max(lo, 0), min(hi, S)
        xe[slo - lo:shi - lo] = x[b, slo:shi]
        tok = lo + np.arange(EXT)
        valid = (tok >= 0) & (tok < S)
        kbias = np.zeros((128, 11), np.float32)
        for j in range(11):
            for p in range(128):
                c = 128 * j + p
                if c < EXT and not valid[c]:
                    kbias[p, j] = NEG
        hflag = np.array([[1.0 if valid[128] else 0.0], [1.0 if valid[1153] else 0.0]], np.float32)
        m = dict(shared)
        m.update({"x_ext": xe, "kbias": kbias, "hflag": hflag})
        in_maps.append(m)
    res = run_bass_kernel_spmd(nc, in_maps, core_ids=list(range(8)))
    out = np.empty((NB, S, D), np.float32)
    for core in range(8):
        b, sc = core // 4, core % 4
        out[b, sc * TOK:(sc + 1) * TOK] = res.results[core]["y"]
    return out
```
